# Optimizing a Trainium2 kernel written in Bass

```python
import math
import jax
import jax.numpy as jnp
from jax import lax
import numpy as np

D_MODEL = 2048
BATCH = 1
SEQ = 16384
DEPTH = 4
DEC_BATCH = 32
DEC_SEQ = 32
PAST_LEN = 4096

CHUNK = 64
EPS = 1e-6
CONV_W = 3
W_CONV = 1024
W_SSM = 1024
SSM_GROUP = 16
SSM_GROUPS = W_SSM // SSM_GROUP
SSM_STATE = 64
SSM_SCAN_BLOCK = 1024
W_SGU = 1024
SGU_CHUNK = 128
SGU_HEADS = 8
SGU_HEAD_DIM = W_SGU // SGU_HEADS
N_BRANCH = 3
D_FF = -(-8 * D_MODEL // (3 * 256)) * 256
SPLITS = (W_CONV, 2 * W_CONV, 3 * W_CONV, 3 * W_CONV + W_SSM, 3 * W_CONV + W_SSM + W_SGU)
D_IN = 3 * W_CONV + W_SSM + 2 * W_SGU

kernel_name = 'hybrid_streaming_encoder_step'


def rms_norm(x, g):
    x32 = x.astype(jnp.float32)
    y = x32 * lax.rsqrt(jnp.mean(x32 * x32, axis=-1, keepdims=True) + EPS)
    return (y * g.astype(jnp.float32)).astype(x.dtype)


def layer_norm(x, g):
    x32 = x.astype(jnp.float32)
    xc = x32 - jnp.mean(x32, axis=-1, keepdims=True)
    y = xc * lax.rsqrt(jnp.mean(xc * xc, axis=-1, keepdims=True) + EPS)
    return (y * g.astype(jnp.float32)).astype(x.dtype)


def _cplx(re, im):
    return lax.complex(re.astype(jnp.float32), im.astype(jnp.float32))


def short_conv(z, buf, w):
    L = z.shape[1]
    zp = jnp.concatenate([buf.astype(z.dtype), z], axis=1)
    y = sum(w[k] * zp[:, k:k + L] for k in range(CONV_W))
    return y, zp[:, L:]


def s5_discretize(lam_re, lam_im, log_dt, b_re, b_im):
    lam = _cplx(lam_re, lam_im)
    dt = jnp.exp(log_dt.astype(jnp.float32))[:, None]
    lam_bar = jnp.exp(lam * dt)
    b_bar = ((lam_bar - 1.0) / lam)[:, :, None] * _cplx(b_re, b_im)
    return lam_bar, b_bar


def _affine_combine(left, right):
    a_l, b_l = left
    a_r, b_r = right
    return a_r * a_l, a_r * b_l + b_r


def s5_scan(u, s0, lam_bar, b_bar, c, d):
    nb, L, _ = u.shape
    blk = math.gcd(L, SSM_SCAN_BLOCK)
    ub = u.reshape(nb, L // blk, blk, SSM_GROUPS, SSM_GROUP).transpose(1, 0, 2, 3, 4)

    def block_step(s, u_blk):
        bu = jnp.einsum('blgi,gpi->blgp', u_blk.astype(jnp.complex64), b_bar)
        a = jnp.broadcast_to(lam_bar, bu.shape)
        a_cum, h = lax.associative_scan(_affine_combine, (a, bu), axis=1)
        h = h + a_cum * s[:, None]
        y = jnp.einsum('blgp,gip->blgi', h, c).real
        return h[:, -1], y

    s_last, y = lax.scan(block_step, s0, ub)
    y = y.transpose(1, 0, 2, 3, 4).reshape(nb, L, W_SSM)
    return y + d.astype(jnp.float32) * u, s_last


def spatial_gate(u, v, w_s, b_s):
    nb, L, _ = v.shape
    n = -(-L // SGU_CHUNK)
    vp = jnp.pad(v, ((0, 0), (0, n * SGU_CHUNK - L), (0, 0)))
    vp = vp.reshape(nb, n, SGU_CHUNK, SGU_HEADS, SGU_HEAD_DIM)
    mask = jnp.tril(jnp.ones((SGU_CHUNK, SGU_CHUNK), dtype=bool))
    w = jnp.where(mask, w_s, 0)
    mix = jnp.einsum('hts,bnshd->bnthd', w, vp) + b_s.T[:, :, None]
    mix = mix.reshape(nb, n * SGU_CHUNK, W_SGU)[:, :L]
    return u * mix


def mixer(xn, conv_buf, s0, w_in, conv_w, w_conv_out, lam_bar, b_bar, c_ssm, d_ssm,
          w_glu, b_glu, w_ssm_out, ln_v_g, w_sgu_s, b_sgu_s, w_sgu_out, w_gate, b_gate, w_o):
    h = xn @ w_in
    b_g, c_g, h_c, u_s, u_g, v_g = jnp.split(h, SPLITS, axis=-1)
    z_conv, conv_new = short_conv(c_g * h_c, conv_buf, conv_w)
    y_a = (b_g * z_conv) @ w_conv_out
    y_s, s_new = s5_scan(u_s.astype(jnp.float32), s0, lam_bar, b_bar, c_ssm, d_ssm)
    y_s = jax.nn.gelu(y_s).astype(xn.dtype)
    y_s = y_s * jax.nn.sigmoid(y_s @ w_glu + b_glu)
    y_b = y_s @ w_ssm_out
    v_n = layer_norm(jax.nn.gelu(v_g), ln_v_g)
    y_c = spatial_gate(jax.nn.gelu(u_g), v_n, w_sgu_s, b_sgu_s) @ w_sgu_out
    gates = jax.nn.sigmoid(jnp.einsum('bld,dke->blke', xn, w_gate) + b_gate)
    merged = gates[:, :, 0] * y_a + gates[:, :, 1] * y_b + gates[:, :, 2] * y_c
    return merged @ w_o, conv_new, s_new, v_n


def swiglu(xn, w_ffn_in, w_ffn_out):
    g, u = jnp.split(xn @ w_ffn_in, 2, axis=-1)
    return (jax.nn.silu(g) * u) @ w_ffn_out


def setup_inputs(seed: int = 0) -> dict:
    key = jax.random.key(seed)
    ks = iter(jax.random.split(key, 40))
    f32 = jnp.float32

    def nrm(shape, scale):
        return jax.random.normal(next(ks), shape, f32) * scale

    n_idx = jnp.arange(SSM_STATE, dtype=f32)
    return {
        'x_prompt': nrm((BATCH, SEQ, D_MODEL), 1.0),
        'x_sample': nrm((DEC_BATCH, DEC_SEQ, D_MODEL), 1.0),
        'cache_conv': nrm((DEPTH, DEC_BATCH, CONV_W - 1, W_CONV), 1.0),
        'state_ssm_re': nrm((DEPTH, DEC_BATCH, SSM_GROUPS, SSM_STATE), 0.5),
        'state_ssm_im': nrm((DEPTH, DEC_BATCH, SSM_GROUPS, SSM_STATE), 0.5),
        'norm_mix_g': 1.0 + nrm((DEPTH, D_MODEL), 0.01),
        'w_in': nrm((DEPTH, D_MODEL, D_IN), D_MODEL ** -0.5),
        'conv_w': nrm((DEPTH, CONV_W, W_CONV), 0.5),
        'w_conv_out': nrm((DEPTH, W_CONV, D_MODEL), W_CONV ** -0.5),
        'ssm_lam_re': -0.5 + nrm((DEPTH, SSM_GROUPS, SSM_STATE), 0.01),
        'ssm_lam_im': jnp.pi * n_idx + nrm((DEPTH, SSM_GROUPS, SSM_STATE), 0.01),
        'ssm_log_dt': jax.random.uniform(next(ks), (DEPTH, SSM_GROUPS), f32, math.log(1e-3), math.log(1e-1)),
        'ssm_b_re': nrm((DEPTH, SSM_GROUPS, SSM_STATE, SSM_GROUP), (2 * SSM_GROUP) ** -0.5),
        'ssm_b_im': nrm((DEPTH, SSM_GROUPS, SSM_STATE, SSM_GROUP), (2 * SSM_GROUP) ** -0.5),
        'ssm_c_re': nrm((DEPTH, SSM_GROUPS, SSM_GROUP, SSM_STATE), (2 * SSM_STATE) ** -0.5),
        'ssm_c_im': nrm((DEPTH, SSM_GROUPS, SSM_GROUP, SSM_STATE), (2 * SSM_STATE) ** -0.5),
        'ssm_d': nrm((DEPTH, W_SSM), 1.0),
        'w_glu': nrm((DEPTH, W_SSM, W_SSM), W_SSM ** -0.5),
        'b_glu': nrm((DEPTH, W_SSM), 0.01),
        'w_ssm_out': nrm((DEPTH, W_SSM, D_MODEL), W_SSM ** -0.5),
        'ln_v_g': 1.0 + nrm((DEPTH, W_SGU), 0.01),
        'w_sgu_s': nrm((DEPTH, SGU_HEADS, SGU_CHUNK, SGU_CHUNK), 0.5 * SGU_CHUNK ** -0.5),
        'b_sgu_s': 1.0 + nrm((DEPTH, SGU_HEADS, SGU_CHUNK), 0.01),
        'w_sgu_out': nrm((DEPTH, W_SGU, D_MODEL), W_SGU ** -0.5),
        'w_gate': nrm((DEPTH, D_MODEL, N_BRANCH, D_MODEL), D_MODEL ** -0.5),
        'b_gate': nrm((DEPTH, N_BRANCH, D_MODEL), 0.01),
        'w_o': nrm((DEPTH, D_MODEL, D_MODEL), D_MODEL ** -0.5),
        'norm_ffn_g': 1.0 + nrm((DEPTH, D_MODEL), 0.01),
        'w_ffn_in': nrm((DEPTH, D_MODEL, 2 * D_FF), D_MODEL ** -0.5),
        'w_ffn_out': nrm((DEPTH, D_FF, D_MODEL), D_FF ** -0.5),
        'norm_final_g': 1.0 + nrm((D_MODEL,), 0.01),
    }


def reference(x_prompt, x_sample, cache_conv, state_ssm_re, state_ssm_im, norm_mix_g, w_in, conv_w,
              w_conv_out, ssm_lam_re, ssm_lam_im, ssm_log_dt, ssm_b_re, ssm_b_im, ssm_c_re, ssm_c_im,
              ssm_d, w_glu, b_glu, w_ssm_out, ln_v_g, w_sgu_s, b_sgu_s, w_sgu_out, w_gate, b_gate, w_o,
              norm_ffn_g, w_ffn_in, w_ffn_out, norm_final_g):
    xp, xs = x_prompt, x_sample
    nbp = x_prompt.shape[0]
    conv_p, re_p, im_p, conv_s, re_s, im_s, v_s = [], [], [], [], [], [], []
    for l in range(DEPTH):
        lam_bar, b_bar = s5_discretize(ssm_lam_re[l], ssm_lam_im[l], ssm_log_dt[l], ssm_b_re[l], ssm_b_im[l])
        c_ssm = _cplx(ssm_c_re[l], ssm_c_im[l])

        def layer(x, conv_buf, s0):
            out, conv_new, s_new, v_n = mixer(
                rms_norm(x, norm_mix_g[l]), conv_buf, s0, w_in[l], conv_w[l], w_conv_out[l],
                lam_bar, b_bar, c_ssm, ssm_d[l], w_glu[l], b_glu[l], w_ssm_out[l], ln_v_g[l],
                w_sgu_s[l], b_sgu_s[l], w_sgu_out[l], w_gate[l], b_gate[l], w_o[l])
            x = x + out
            x = x + swiglu(rms_norm(x, norm_ffn_g[l]), w_ffn_in[l], w_ffn_out[l])
            return x, conv_new, s_new, v_n

        xp, cp, sp, _ = layer(xp, jnp.zeros((nbp, CONV_W - 1, W_CONV), xp.dtype),
                              jnp.zeros((nbp, SSM_GROUPS, SSM_STATE), jnp.complex64))
        xs, cs, ss, vs = layer(xs, cache_conv[l], _cplx(state_ssm_re[l], state_ssm_im[l]))
        conv_p.append(cp)
        re_p.append(sp.real)
        im_p.append(sp.imag)
        conv_s.append(cs)
        re_s.append(ss.real)
        im_s.append(ss.imag)
        v_s.append(vs)

    y_prompt = rms_norm(xp, norm_final_g)
    y_sample = rms_norm(xs, norm_final_g)
    sdt = state_ssm_re.dtype
    new_conv_prompt = jnp.stack(conv_p)
    new_ssm_re_prompt = jnp.stack(re_p).astype(sdt)
    new_ssm_im_prompt = jnp.stack(im_p).astype(sdt)
    new_conv_sample = jnp.stack(conv_s)
    new_ssm_re_sample = jnp.stack(re_s).astype(sdt)
    new_ssm_im_sample = jnp.stack(im_s).astype(sdt)
    new_sgu_v_sample = jnp.stack(v_s)
    return (y_prompt, y_sample, new_conv_prompt, new_ssm_re_prompt, new_ssm_im_prompt,
            new_conv_sample, new_ssm_re_sample, new_ssm_im_sample, new_sgu_v_sample)
```

```python
import math
from contextlib import ExitStack

import numpy as np
import concourse.bass as bass
import concourse.mybir as mybir
from concourse.bass_utils import run_bass_kernel_spmd

F32 = mybir.dt.float32
BF16 = mybir.dt.bfloat16
AF = mybir.ActivationFunctionType
ALU = mybir.AluOpType
AX = mybir.AxisListType

import os
NCORE = 8
RUN_DEPTH = int(os.environ.get('KDEPTH', '4'))
KSTOP = float(os.environ.get('KSTOP', '99'))


class StopBuild(Exception):
    pass

D = 2048
DT = 16
DEPTH = 4
TC = 2176
NB = 4
BW = 544
HW = 272
NCH = 68
DFF = 5632
EPS = 1e-6
NW = 3
ND = 8
ENGS = ['pe', 'act', 'dve', 'pool', 'sp']

V_G1, V_G2, V_CW, V_D, V_BGLU, V_BGATE, NV = 0, 16, 32, 56, 64, 72, 120


class Res:
    __slots__ = ('w', 'r')

    def __init__(self):
        self.w = None
        self.r = {}


def pieces(c0, c1):
    out = []
    if c0 < HW:
        e = min(c1, HW)
        out.append((0, c0, e, c0, e))
    if c1 > HW:
        s = max(c0, HW)
        out.append((1, s - HW, c1 - HW, s, c1))
    return out


class Bld:
    def __init__(self, nc, dry, plan):
        self.nc = nc
        self.dry = dry
        self.plan = plan
        self.pidx = 0
        self.issued = 0
        self.eng = {'pe': nc.tensor, 'act': nc.scalar, 'dve': nc.vector, 'pool': nc.gpsimd, 'sp': nc.sync}
        self.cnt = {e: 0 for e in ENGS}
        self.seen = {e: {} for e in ENGS}
        self.sem = {}
        self.semval = {}
        self.dq = {'sp': 0, 'pool': 0}
        self.es = ExitStack()
        for e in ENGS:
            self.sem[e] = self.es.enter_context(nc.semaphore('s_' + e))
        for q in ('sp', 'pool'):
            for i in range(ND):
                self.sem['d%s%d' % (q, i)] = self.es.enter_context(nc.semaphore('d%s%d' % (q, i)))
        self.sem['cc'] = self.es.enter_context(nc.semaphore('cc'))
        self.ccn = 0
        self.acc_i = 0
        self.stopped = False

    def _wait(self, e, key, val):
        if self.stopped:
            return
        if self.seen[e].get(key, 0) >= val:
            return
        self.seen[e][key] = val
        if not self.dry:
            self.eng[e].wait_ge(self.sem[key], val)

    def _deps(self, e, R, W):
        for r in R:
            if r.w is not None:
                k, v = r.w
                if not (e == 'pe' and k == 'pe'):
                    self._wait(e, k, v)
        for w in W:
            if w.w is not None:
                k, v = w.w
                if not (e == 'pe' and k == 'pe'):
                    self._wait(e, k, v)
            for k, v in w.r.items():
                if not (e == 'pe' and k == 'pe'):
                    self._wait(e, k, v)

    def _upd(self, evt, R, W):
        k, v = evt
        self.semval[k] = max(self.semval.get(k, 0), v)
        for r in R:
            if r.r.get(k, 0) < v:
                r.r[k] = v
        for w in W:
            w.w = evt
            w.r = {}

    def op(self, e, fn, R=(), W=(), mark=True):
        if self.stopped:
            return
        self._deps(e, R, W)
        if mark:
            self.cnt[e] += 1
            evt = (e, self.cnt[e])
        else:
            evt = (e, self.cnt[e] + 1)
        if not self.dry:
            ins = fn(self.eng[e])
            if mark:
                ins.then_inc(self.sem[e], 1)
        self._upd(evt, R, W)

    def dma(self, q, out, in_, R=(), W=()):
        if self.stopped:
            return
        i = self.dq[q]
        self.dq[q] += 1
        key = 'd%s%d' % (q, i % ND)
        val = 16 * (i // ND + 1)
        if i >= ND:
            self._wait(q, key, val - 16)
        self._deps(q, R, W)
        if not self.dry:
            self.eng[q].dma_start(out=out, in_=in_).then_inc(self.sem[key], 16)
        self._upd((key, val), R, W)

    def barrier(self):
        for e in ENGS:
            for k, v in self.semval.items():
                if not (e == 'pe' and k == 'pe'):
                    self._wait(e, k, v)

    def mm(self, out, lhsT, rhs, start, stop, R=(), W=(), mark=False):
        self.op('pe', lambda pe: pe.matmul(out, lhsT=lhsT, rhs=rhs, start=start, stop=stop), R=R, W=W, mark=mark)

    def tt(self, e, out, in0, in1, op, R=(), W=()):
        self.op(e, lambda g: g.tensor_tensor(out=out, in0=in0, in1=in1, op=op), R=R, W=W)

    def ts(self, e, out, in0, s1, s2, op0, op1=None, R=(), W=()):
        if op1 is None:
            self.op(e, lambda g: g.tensor_scalar(out=out, in0=in0, scalar1=s1, scalar2=None, op0=op0), R=R, W=W)
        else:
            self.op(e, lambda g: g.tensor_scalar(out=out, in0=in0, scalar1=s1, scalar2=s2, op0=op0, op1=op1), R=R, W=W)

    def stt(self, e, out, in0, scalar, in1, op0, op1, R=(), W=()):
        self.op(e, lambda g: g.scalar_tensor_tensor(out=out, in0=in0, scalar=scalar, in1=in1, op0=op0, op1=op1), R=R, W=W)

    def act(self, out, in_, func, bias=None, scale=None, R=(), W=()):
        kw = {}
        if bias is not None:
            kw['bias'] = bias
        if scale is not None:
            kw['scale'] = scale
        self.op('act', lambda g: g.activation(out=out, in_=in_, func=func, **kw), R=R, W=W)

    def cp(self, e, out, in_, R=(), W=()):
        if e == 'act':
            self.op('act', lambda g: g.copy(out=out, in_=in_), R=R, W=W)
        else:
            self.op(e, lambda g: g.tensor_copy(out=out, in_=in_), R=R, W=W)

    def sb(self, scope, name, shape, dt):
        self.nalloc = getattr(self, 'nalloc', 0) + 1
        return scope.enter_context(self.nc.sbuf_tensor("%s_%d" % (name, self.nalloc), shape, dt))

    def view(self, t, off, dims, p0=0, npart=128):
        a = t[:] if not isinstance(t, bass.AP) else t
        pstep = a.ap[0][0]
        return bass.AP(a.tensor, a.offset + p0 * pstep + off, [[pstep, npart]] + [list(d) for d in dims])


def _build(nc, dry, plan):
    b = Bld(nc, dry, plan)
    es = b.es

    def din(name, shape):
        return nc.dram_tensor(name, list(shape), F32, kind="ExternalInput").ap()

    def dout(name, shape):
        return nc.dram_tensor(name, list(shape), F32, kind="ExternalOutput").ap()

    xT = din("xT", [DT, 128, TC])
    wshapes = {'in': ("w_in", D, 6144), 'gate': ("w_gate", D, 6144), 'cvo': ("w_conv_out", 1024, D),
               'glu': ("w_glu", 1024, 1024), 'sso': ("w_ssm_out", 1024, D), 'sgo': ("w_sgu_out", 1024, D),
               'o': ("w_o", D, D), 'fi': ("w_ffn_in", D, 2 * DFF), 'fo': ("w_ffn_out", DFF, D)}
    if KSTOP <= 4:
        wshapes = {k: (v if k == 'in' else (v[0], 128, 256)) for k, v in wshapes.items()}
    Wd = {k: [din("%s_%d" % (nm, l_), [kk, nn]) for l_ in range(RUN_DEPTH)] for k, (nm, kk, nn) in wshapes.items()}
    vecs_d = din("vecs", [DEPTH, 128, NV])
    gF_d = din("gF", [128, DT])
    lnv_d = din("lnv", [DEPTH, 128, 1024])
    bsgu_d = din("bsgu", [DEPTH, 128, 8, 128])
    wsgT_d = din("wsgT", [DEPTH, 128, 8, 128])
    cst_d = din("cst", [128, 4, 128])
    lam_d = din("lam", [DEPTH, 128, 2, 64])
    ldt_d = din("ldt", [DEPTH, 128, 64])
    B1_d = din("B1", [DEPTH, 128, 1024])
    B2_d = din("B2", [DEPTH, 128, 1024])
    C1_d = din("C1", [DEPTH, 128, 1024])
    C2_d = din("C2", [DEPTH, 128, 1024])
    convc_d = din("convc", [DEPTH, NB, 128, 8, 2])
    st0_d = din("st0", [DEPTH, 2, 64, 64, NB])
    cmask_d = din("cmask", [128, 16])

    yT = dout("yT", [DT, 128, TC])
    convp_o = dout("convp", [DEPTH, 128, 16])
    convs_o = dout("convs", [DEPTH, NB, 128, 8, 2])
    ssmp_o = dout("ssmp", [DEPTH, 2, 64, 64])
    ssms_o = dout("ssms", [DEPTH, 2, 64, 64, NB])
    sguv_o = dout("sguv", [DEPTH, NB, 32, 1024])

    XS = nc.dram_tensor("xs", [DT, 128, TC], F32).ap()
    EXI = nc.dram_tensor("exi", [128, 80], F32)
    EXO = nc.dram_tensor("exo", [NCORE * 128, 80], F32)
    XSr = [[Res() for _ in range(DT)] for _ in range(NB)]
    EXIr, EXOr = Res(), Res()

    sb = b.sb
    PS = es.enter_context(nc.psum_tensor("PS", [128, 8, 512], F32))
    WS = [sb(es, "WS%d" % i, [128, 16, 256], BF16) for i in range(NW)]
    WR = [Res() for _ in range(NW)]
    YU = [sb(es, "YU%d" % c, [128, TC], BF16) for c in range(8)]
    YUr = [Res() for _ in range(8)]
    CST = sb(es, "CST", [128, 4, 128], F32)
    TRIb = sb(es, "TRIb", [128, 128], BF16)
    IDCM = sb(es, "IDCM", [128, 8, 256], BF16)
    ONESB = sb(es, "ONESB", [128, 128], BF16)
    BD4 = sb(es, "BD4", [128, 4, 128], F32)
    GF = sb(es, "GF", [128, DT], F32)
    CMK = sb(es, "CMK", [128, 16], F32)
    VEC = sb(es, "VEC", [128, NV], F32)
    ZH = sb(es, "ZH", [128, 8, 2], F32)
    ZL = sb(es, "ZL", [128, 16], F32)
    KC = sb(es, "KC", [128, 8], F32)
    cR = {k: Res() for k in ('cst', 'tri', 'idcm', 'ones', 'gf', 'cmk', 'vec', 'lng', 'bsb', 'wtm', 'zh', 'zl', 'kc')}
    ACCr = [Res() for _ in range(4)]

    def acc_next():
        i = b.acc_i
        b.acc_i = (i + 1) % 4
        return i

    def accv(s):
        return b.view(PS, s * 1024, [[512, 2], [1, HW]])

    def acch(s, h, lo, hi):
        return b.view(PS, s * 1024 + h * 512 + lo, [[1, hi - lo]])

    def accflat(s, n):
        return b.view(PS, s * 1024, [[1, n]])

    def hv(t, off=0):
        return b.view(t, off, [[HW, 2], [1, HW]])

    b.dma('sp', CST[:], cst_d[:], W=[cR['cst']])
    b.dma('sp', GF[:], gF_d[:], W=[cR['gf']])
    b.dma('sp', CMK[:], cmask_d[:], W=[cR['cmk']])
    b.cp('dve', TRIb[:], CST[:, 0, :], R=[cR['cst']], W=[cR['tri']])
    b.op('dve', lambda g: g.memset(ONESB[:], 1.0 / D), W=[cR['ones']])
    b.op('dve', lambda g: g.memset(KC[:, 0:1], EPS), W=[cR['kc']])
    b.op('dve', lambda g: g.memset(KC[:, 1:2], math.pi / 2), W=[cR['kc']])
    b.op('dve', lambda g: g.memset(KC[:, 2:3], 0.0), W=[cR['kc']])
    b.op('dve', lambda g: g.memset(KC[:, 3:4], 1.0), W=[cR['kc']])
    BDm = CST[:, 1, :]
    IDf = CST[:, 2, :]
    MCOL = lambda gg: CST[:, 3, gg:gg + 1]
    SGN = CST[:, 3, 8:9]
    EPSc = KC[:, 0:1]
    HPIc = KC[:, 1:2]
    for c in range(8):
        b.cp('dve', IDCM[:, c, 0:128], IDf, R=[cR['cst']], W=[cR['idcm']])
    for c in range(4):
        b.cp('dve', BD4[:, c, :], BDm, R=[cR['cst']], W=[cR['idcm']])

    def issue(j):
        name, l, kt0, nkt, n0, ncols = b.plan[j]
        s = j % NW
        src = Wd[name][l][kt0 * 128:(kt0 + nkt) * 128, n0:n0 + ncols].rearrange("(kt p) n -> p kt n", p=128)
        b.dma('pool', WS[s][:, 0:nkt, 0:ncols], src, W=[WR[s]])

    def need(name, l, kt0, nkt, n0, ncols=256):
        desc = (name, l, kt0, nkt, n0, ncols)
        if b.stopped:
            return WS[0], WR[0]
        if b.dry:
            b.plan.append(desc)
            return WS[0], WR[0]
        assert b.plan[b.pidx] == desc, (b.plan[b.pidx], desc)
        while b.issued < min(len(b.plan), b.pidx + NW):
            issue(b.issued)
            b.issued += 1
        s = b.pidx % NW
        b.pidx += 1
        return WS[s], WR[s]

    def proj(a, wt, wr, col, xs, xr, nkt, wk0=0, first=True, last=True, split=HW):
        for kt in range(nkt):
            for h in range(2):
                fin = last and kt == nkt - 1 and h == 1
                cs, ce = (0, split) if h == 0 else (split, BW)
                b.mm(acch(a, h, 0, ce - cs), wt[:, wk0 + kt, col:col + 128], xs[kt][:, cs:ce],
                     start=(first and kt == 0), stop=(last and kt == nkt - 1),
                     R=[wr] + list(xr), W=[ACCr[a]], mark=fin)

    def stop(n):
        if KSTOP <= n and not b.stopped:
            b.barrier()
            b.stopped = True

    def body():
      for l in range(RUN_DEPTH):
        last_layer = (l == RUN_DEPTH - 1)
        b.dma('sp', VEC[:], vecs_d[l], W=[cR['vec']])
        G1 = lambda ft: VEC[:, V_G1 + ft:V_G1 + ft + 1]
        G2 = lambda ft: VEC[:, V_G2 + ft:V_G2 + ft + 1]
        CW = lambda k, ct: VEC[:, V_CW + k * 8 + ct:V_CW + k * 8 + ct + 1]
        DV = lambda ct: VEC[:, V_D + ct:V_D + ct + 1]
        BGLU = lambda ct: VEC[:, V_BGLU + ct:V_BGLU + ct + 1]
        BGATE = lambda e, ft: VEC[:, V_BGATE + e * 16 + ft:V_BGATE + e * 16 + ft + 1]

        xsrc = xT if l == 0 else XS

        with ExitStack() as sp_:
            X32 = [sb(sp_, "X32_%d" % i, [128, BW], F32) for i in range(DT)]
            X32r = [Res() for _ in range(DT)]
            XN = [sb(sp_, "XN_%d" % i, [128, BW], BF16) for i in range(DT)]
            XNr = [Res() for _ in range(DT)]
            SQ = [sb(sp_, "SQ%d" % i, [128, BW], BF16) for i in range(2)]
            SQr = [Res(), Res()]
            RS = sb(sp_, "RS", [128, BW], F32)
            RSr = Res()

            def rstd_from_acc(a):
                b.act(hv(RS), accv(a), AF.Sqrt, bias=EPSc, R=[ACCr[a], cR['kc']], W=[RSr])
                b.op('dve', lambda g: g.reciprocal(out=RS[:], in_=RS[:]), R=[RSr], W=[RSr])

            def sumsq_to_acc(src, srcr):
                a = acc_next()
                for ft in range(DT):
                    q = ft % 2
                    b.act(SQ[q][:], src[ft][:], AF.Square, R=[srcr[ft]], W=[SQr[q]])
                    for h in range(2):
                        b.mm(acch(a, h, 0, HW), ONESB[:], SQ[q][:, h * HW:(h + 1) * HW], start=(ft == 0), stop=(ft == DT - 1),
                             R=[SQr[q], cR['ones']], W=[ACCr[a]], mark=(h == 1))
                return a

            def load_norm(bk, gfun):
                for ft in range(DT):
                    rr = [XSr[bk][ft]] if l > 0 else []
                    b.dma('sp', X32[ft][:], xsrc[ft, :, bk * BW:(bk + 1) * BW], R=rr, W=[X32r[ft]])
                a = sumsq_to_acc(X32, X32r)
                rstd_from_acc(a)
                for ft in range(DT):
                    b.stt('dve', XN[ft][:], X32[ft][:], gfun(ft), RS[:], ALU.mult, ALU.mult,
                          R=[X32r[ft], RSr, cR['vec']], W=[XNr[ft]])

            for bk in range(NB):
                load_norm(bk, G1)
                for pr in range(4):
                    wt, wr = need('in', l, 0, 16, 3072 + pr * 256)
                    for t2 in range(2):
                        ct = pr * 2 + t2
                        a = acc_next()
                        proj(a, wt, wr, t2 * 128, XN, XNr, 16, split=256)
                        for h in range(2):
                            src = b.view(PS, a * 1024 + h * 512, [[32, 8], [1, 32]])
                            dst = b.view(YU[ct], 16 * bk + 8 * h, [[1, 8], [NCH, 32]])
                            b.cp('act' if h == 0 else 'dve', dst, src, R=[ACCr[a]], W=[YUr[ct]])
                        src = acch(a, 1, 256, 288)
                        dst = b.view(YU[ct], 64 + bk, [[NCH, 32]])
                        b.cp('dve', dst, src, R=[ACCr[a]], W=[YUr[ct]])
                if bk == NB - 1:
                    a = acc_next()
                    xl = [XN[kt][:, 510:512] for kt in range(DT)]
                    for which in range(2):
                        for pr in range(4):
                            wt, wr = need('in', l, 0, 16, 1024 * (1 + which) + pr * 256)
                            for t2 in range(2):
                                ct = pr * 2 + t2
                                o = b.view(PS, a * 1024 + which * 16 + ct * 2, [[1, 2]])
                                for kt in range(DT):
                                    b.mm(o, wt[:, kt, t2 * 128:t2 * 128 + 128], xl[kt], start=(kt == 0), stop=(kt == DT - 1),
                                         R=[wr] + XNr, W=[ACCr[a]], mark=(kt == DT - 1))
                    with ExitStack() as sz:
                        CGt = sb(sz, "CGt", [128, 16], BF16)
                        cgr = Res()
                        b.cp('act', CGt[:], accflat(a, 16), R=[ACCr[a]], W=[cgr])
                        b.tt('dve', ZL[:], b.view(PS, a * 1024 + 16, [[1, 16]]), CGt[:], ALU.mult, R=[ACCr[a], cgr], W=[cR['zl']])
                        b.barrier()
            b.barrier()
            stop(1)
        with ExitStack() as ss:
            LAM = sb(ss, "LAM", [128, 2, 64], F32)
            LDT = sb(ss, "LDT", [128, 64], F32)
            POWr = sb(ss, "POWr", [128, 33, 64], F32)
            POWi = sb(ss, "POWi", [128, 33, 64], F32)
            BB1 = sb(ss, "BB1", [128, 1024], F32)
            BB2 = sb(ss, "BB2", [128, 1024], F32)
            CMf = sb(ss, "CMf", [128, 1024], F32)
            CC2 = sb(ss, "CC2", [128, 1024], F32)
            SSB = sb(ss, "SSB", [128, 64, NCH], F32)
            IMT = sb(ss, "IMT", [64, 64, NCH], F32)
            SP = sb(ss, "SP", [128, 64, NCH], BF16)
            SPI = sb(ss, "SPI", [64, 64, NCH], BF16)
            T = [sb(ss, "T%d" % i, [128, 64], F32) for i in range(12)]
            ST = [sb(ss, "ST%d" % i, [64, 64], F32) for i in range(4)]
            SF = sb(ss, "SF", [128, 8, 80], F32)
            SFI = sb(ss, "SFI", [64, 8, 64], F32)
            L2K = sb(ss, "L2K", [64, 2, 64], F32)
            S0 = sb(ss, "S0", [64, 2, 64, NB], F32)
            SN = sb(ss, "SN", [64, 2, 64, NB], F32)
            r_ = {k: Res() for k in ('lam', 'pow', 'bb', 'cm', 'ssb', 'imt', 'sp', 'spi', 't', 'st', 'sf', 'sfi', 'l2k', 's0', 'sn')}
            rt = r_['t']

            b.dma('sp', LAM[:], lam_d[l], W=[r_['lam']])
            b.dma('sp', LDT[:], ldt_d[l], W=[r_['lam']])
            b.dma('sp', BB1[:], B1_d[l], W=[r_['bb']])
            b.dma('sp', BB2[:], B2_d[l], W=[r_['bb']])
            b.dma('sp', CMf[:], C1_d[l], W=[r_['cm']])
            b.dma('sp', CC2[:], C2_d[l], W=[r_['cm']])
            b.dma('sp', S0[:], st0_d[l].rearrange("r p g s -> p r g s"), W=[r_['s0']])

            def T_(i):
                return T[i][:]

            def tt_(out, a0, a1, op, R=(), W=()):
                b.tt('dve', out, a0, a1, op, R=[rt] + list(R), W=[rt] + list(W))

            lre, lim = LAM[:, 0, :], LAM[:, 1, :]
            b.act(T_(0), LDT[:], AF.Exp, R=[r_['lam']], W=[rt])
            tt_(T_(1), lre, T_(0), ALU.mult, R=[r_['lam']])
            tt_(T_(2), lim, T_(0), ALU.mult, R=[r_['lam']])
            b.act(T_(3), T_(1), AF.Exp, R=[rt], W=[rt])
            b.act(T_(4), T_(2), AF.Sin, scale=1.0 / 16, R=[rt], W=[rt])
            b.act(T_(5), T_(2), AF.Sin, bias=HPIc, scale=1.0 / 16, R=[rt, cR['kc']], W=[rt])
            for _ in range(4):
                tt_(T_(6), T_(5), T_(5), ALU.mult)
                tt_(T_(7), T_(4), T_(4), ALU.mult)
                tt_(T_(8), T_(5), T_(4), ALU.mult)
                tt_(T_(5), T_(6), T_(7), ALU.subtract)
                tt_(T_(4), T_(8), T_(8), ALU.add)
            rp = r_['pow']
            b.op('dve', lambda g: g.memset(POWr[:, 0, :], 1.0), W=[rp])
            b.op('dve', lambda g: g.memset(POWi[:, 0, :], 0.0), W=[rp])
            b.tt('dve', POWr[:, 1, :], T_(5), T_(3), ALU.mult, R=[rt], W=[rp])
            b.tt('dve', POWi[:, 1, :], T_(4), T_(3), ALU.mult, R=[rt], W=[rp])
            with ExitStack() as sg:
                PT = [sb(sg, "PT%d" % i, [128, 16, 64], F32) for i in range(2)]
                ptr = Res()
                k = 1
                while k < 32:
                    n = min(k, 32 - k)
                    ar, ai = POWr[:, 1:1 + n, :], POWi[:, 1:1 + n, :]
                    br_ = b.view(POWr, k * 64, [[0, n], [1, 64]])
                    bi_ = b.view(POWi, k * 64, [[0, n], [1, 64]])
                    o_r, o_i = POWr[:, k + 1:k + 1 + n, :], POWi[:, k + 1:k + 1 + n, :]
                    b.tt('dve', PT[0][:, 0:n, :], ar, br_, ALU.mult, R=[rp], W=[ptr])
                    b.tt('dve', PT[1][:, 0:n, :], ai, bi_, ALU.mult, R=[rp], W=[ptr])
                    b.tt('dve', o_r, PT[0][:, 0:n, :], PT[1][:, 0:n, :], ALU.subtract, R=[ptr], W=[rp])
                    b.tt('dve', PT[0][:, 0:n, :], ar, bi_, ALU.mult, R=[rp], W=[ptr])
                    b.tt('dve', PT[1][:, 0:n, :], ai, br_, ALU.mult, R=[rp], W=[ptr])
                    b.tt('dve', o_i, PT[0][:, 0:n, :], PT[1][:, 0:n, :], ALU.add, R=[ptr], W=[rp])
                    k += n
                b.barrier()
            stop(1.1)
            b.cp('dve', L2K[:, 0, :], POWr[0:64, 32, :], R=[rp], W=[r_['l2k']])
            b.cp('dve', L2K[:, 1, :], POWi[0:64, 32, :], R=[rp], W=[r_['l2k']])
            t6, t7, t8 = T[6][0:64, :], T[7][0:64, :], T[8][0:64, :]
            for _ in range(6):
                b.tt('dve', t6, L2K[:, 0, :], L2K[:, 0, :], ALU.mult, R=[r_['l2k']], W=[rt])
                b.tt('dve', t7, L2K[:, 1, :], L2K[:, 1, :], ALU.mult, R=[r_['l2k']], W=[rt])
                b.tt('dve', t8, L2K[:, 0, :], L2K[:, 1, :], ALU.mult, R=[r_['l2k']], W=[rt])
                b.tt('dve', L2K[:, 0, :], t6, t7, ALU.subtract, R=[rt], W=[r_['l2k']])
                b.tt('dve', L2K[:, 1, :], t8, t8, ALU.add, R=[rt], W=[r_['l2k']])
            stop(1.2)
            tt_(T_(6), lre, lre, ALU.mult, R=[r_['lam']])
            tt_(T_(7), lim, lim, ALU.mult, R=[r_['lam']])
            tt_(T_(6), T_(6), T_(7), ALU.add)
            b.op('dve', lambda g: g.reciprocal(out=T_(6), in_=T_(6)), R=[rt], W=[rt])
            b.ts('dve', T_(7), POWr[:, 1, :], -1.0, None, ALU.add, R=[rp], W=[rt])
            tt_(T_(8), T_(7), lre, ALU.mult, R=[r_['lam']])
            tt_(T_(9), POWi[:, 1, :], lim, ALU.mult, R=[r_['lam'], rp])
            tt_(T_(8), T_(8), T_(9), ALU.add)
            tt_(T_(8), T_(8), T_(6), ALU.mult)
            tt_(T_(9), POWi[:, 1, :], lre, ALU.mult, R=[r_['lam'], rp])
            tt_(T_(10), T_(7), lim, ALU.mult, R=[r_['lam']])
            tt_(T_(9), T_(9), T_(10), ALU.subtract)
            tt_(T_(9), T_(9), T_(6), ALU.mult)
            b.ts('dve', T_(9), T_(9), SGN, None, ALU.mult, R=[rt, cR['cst']], W=[rt])
            with ExitStack() as sg:
                U1 = sb(sg, "U1", [128, 64, 16], F32)
                U2 = sb(sg, "U2", [128, 64, 16], F32)
                U3 = sb(sg, "U3", [128, 64, 16], F32)
                ur = Res()
                krb = b.view(T[8], 0, [[1, 64], [0, 16]])
                kib = b.view(T[9], 0, [[1, 64], [0, 16]])
                B1v = b.view(BB1, 0, [[16, 64], [1, 16]])
                B2v = b.view(BB2, 0, [[16, 64], [1, 16]])
                b.tt('dve', U1[:], B1v, krb, ALU.mult, R=[r_['bb'], rt], W=[ur])
                b.tt('dve', U2[:], B2v, kib, ALU.mult, R=[r_['bb'], rt], W=[ur])
                b.tt('dve', U3[:], U1[:], U2[:], ALU.subtract, R=[ur], W=[ur])
                b.tt('dve', U1[:], B2v, krb, ALU.mult, R=[r_['bb'], rt], W=[ur])
                b.tt('dve', U2[:], B1v, kib, ALU.mult, R=[r_['bb'], rt], W=[ur])
                b.tt('dve', U1[:], U1[:], U2[:], ALU.add, R=[ur], W=[ur])
                b.cp('dve', B1v, U3[:], R=[ur], W=[r_['bb']])
                b.ts('dve', B2v, U1[:], SGN, None, ALU.mult, R=[ur, cR['cst']], W=[r_['bb']])
                b.barrier()
            b.ts('dve', CMf[:], CMf[:], SGN, None, ALU.mult, R=[cR['cst']], W=[r_['cm']])
            for c in range(8):
                b.cp('dve', IDCM[:, c, 128:256], CMf[:, c * 128:(c + 1) * 128], R=[r_['cm']], W=[cR['idcm']])

            stop(1.3)

            def cplx_tab(out, t1, t2, tr, m0, nm, ct, X1, X2, xr, powi_sgn):
                pr_ = b.view(POWr, m0 * 64 + ct * 8, [[64, nm], [1, 8], [0, 16]])
                pi_ = b.view(POWi, m0 * 64 + ct * 8, [[64, nm], [1, 8], [0, 16]])
                x1 = b.view(X1, ct * 128, [[0, nm], [16, 8], [1, 16]])
                x2 = b.view(X2, ct * 128, [[0, nm], [16, 8], [1, 16]])
                o4 = lambda t: b.view(t, 0, [[128, nm], [16, 8], [1, 16]])
                b.tt('dve', o4(t1), pr_, x1, ALU.mult, R=[rp, xr], W=[tr])
                b.tt('pool', o4(t2), pi_, x2, ALU.mult, R=[rp, xr], W=[tr])
                b.tt('dve', o4(out), o4(t1), o4(t2), ALU.subtract, R=[tr], W=[tr])

            with ExitStack() as sB:
                t1 = sb(sB, "t1", [128, 8, 128], F32)
                t2 = sb(sB, "t2", [128, 8, 128], F32)
                LB = sb(sB, "LB", [128, 32, 128], BF16)
                LBT = sb(sB, "LBT", [128, 32, 128], BF16)
                KT = sb(sB, "KT", [128, 32, 128], BF16)
                UM = [sb(sB, "UM%d" % i, [128, 32 * NCH], BF16) for i in range(2)]
                lbr, lbtr, ktr, t12r = Res(), Res(), Res(), Res()
                umr = [Res(), Res()]
                for ct in range(8):
                    for hf in range(4):
                        cplx_tab(LB[:, hf * 8:(hf + 1) * 8, :], t1, t2, t12r, hf * 8, 8, ct, BB1, BB2, r_['bb'], True)
                    stop(1.4)
                    lbr.w = t12r.w
                    for g4 in range(8):
                        a = acc_next()
                        for i4 in range(4):
                            lag = g4 * 4 + i4
                            o = b.view(PS, a * 1024 + i4 * 256, [[1, 256]])
                            b.mm(o, LB[:, lag, :], IDCM[:, ct, :], start=True, stop=True, R=[t12r, cR['idcm']], W=[ACCr[a]], mark=(i4 == 3))
                        stop(1.41)
                        src_t = b.view(PS, a * 1024, [[256, 4], [1, 128]])
                        src_k = b.view(PS, a * 1024 + 128, [[256, 4], [1, 128]])
                        b.cp('dve', LBT[:, g4 * 4:g4 * 4 + 4, :], src_t, R=[ACCr[a]], W=[lbtr])
                        stop(1.42)
                        b.tt('dve', KT[:, g4 * 4:g4 * 4 + 4, :], src_k, BD4[:], ALU.mult,
                             R=[ACCr[a], cR['idcm']], W=[ktr])
                    stop(1.45)
                    b.stt('dve', KT[:, 0, :], IDCM[:, ct, 0:128], DV(ct), KT[:, 0, :], ALU.mult, ALU.add, R=[cR['idcm'], cR['vec']], W=[ktr])
                    stop(1.5)
                    for gg in range(8):
                        g = ct * 8 + gg
                        u = UM[gg % 2]
                        b.ts('dve', u[:], YU[ct][:], MCOL(gg), None, ALU.mult, R=[YUr[ct], cR['cst']], W=[umr[gg % 2]])
                        if gg % 4 == 0:
                            a = acc_next()
                        o = b.view(PS, a * 1024 + (gg % 4) * NCH, [[1, NCH]])
                        for r in range(32):
                            b.mm(o, LBT[:, 31 - r, :], u[:, r * NCH:(r + 1) * NCH], start=(r == 0), stop=(r == 31),
                                 R=[lbtr, umr[gg % 2]], W=[ACCr[a]], mark=(r == 31))
                        if gg % 4 == 3:
                            src = b.view(PS, a * 1024, [[NCH, 4], [1, NCH]])
                            b.cp('act', SSB[:, g - 3:g + 1, :], src, R=[ACCr[a]], W=[r_['ssb']])
                    stop(1.6)
                    allacc = ACCr
                    for lag in range(32):
                        for bkk in range(5):
                            r0 = max(lag, 7 * bkk)
                            r1 = min(7 * bkk + 7, 32)
                            if r0 >= r1:
                                continue
                            lastlag = r1 - 1
                            o = b.view(PS, bkk * 512 + (r0 - 7 * bkk) * NCH, [[1, (r1 - r0) * NCH]])
                            rhs = YU[ct][:, (r0 - lag) * NCH:(r1 - lag) * NCH]
                            b.mm(o, KT[:, lag, :], rhs, start=(lag == 0), stop=(lag == lastlag),
                                 R=[ktr, YUr[ct]], W=allacc, mark=(lag == 31))
                    for bkk in range(5):
                        nr = min(7, 32 - 7 * bkk)
                        for bq in range(NB):
                            src = b.view(PS, bkk * 512 + 16 * bq, [[NCH, nr], [1, 16]])
                            dst = b.view(YU[ct], BW * bq + 7 * bkk, [[1, nr], [32, 16]])
                            b.cp('act' if bkk % 2 == 0 else 'dve', dst, src, R=allacc, W=[YUr[ct]])
                        src = b.view(PS, bkk * 512 + 64, [[NCH, nr], [1, NB]])
                        dst = b.view(YU[ct], 512 + 7 * bkk, [[1, nr], [BW, NB]])
                        b.cp('act' if bkk % 2 == 0 else 'dve', dst, src, R=allacc, W=[YUr[ct]])
                    stop(1.7)
                b.barrier()
            stop(2)

            b.dma('sp', IMT[:], SSB[64:128, :, :], R=[r_['ssb']], W=[r_['imt']])
            LR, LI = POWr[0:64, 32, :], POWi[0:64, 32, :]
            ta, tb_, tc_, td = T[0][0:64, :], T[1][0:64, :], T[2][0:64, :], T[3][0:64, :]

            def step(re_o, im_o, re_i, im_i, sre, sim, lr=LR, li=LI, R=()):
                R = [rt, rp, r_['st'], r_['ssb'], r_['imt']] + list(R)
                W = [rt, r_['st']]
                b.tt('dve', ta, re_i, lr, ALU.mult, R=R, W=W)
                b.tt('dve', tb_, im_i, li, ALU.mult, R=R, W=W)
                b.tt('dve', tc_, im_i, lr, ALU.mult, R=R, W=W)
                b.tt('dve', td, re_i, li, ALU.mult, R=R, W=W)
                b.tt('dve', ta, ta, tb_, ALU.subtract, R=R, W=W)
                b.tt('dve', tc_, tc_, td, ALU.add, R=R, W=W)
                b.tt('dve', re_o, ta, sre, ALU.add, R=R, W=W)
                b.tt('dve', im_o, tc_, sim, ALU.add, R=R, W=W)

            SRE = lambda c: SSB[0:64, :, c]
            SIM = lambda c: IMT[:, :, c]
            b.cp('dve', ST[0][:], SRE(0), R=[r_['ssb']], W=[r_['st']])
            b.cp('dve', ST[1][:], SIM(0), R=[r_['imt']], W=[r_['st']])
            cur = 0
            for c in range(1, 64):
                nxt = 1 - cur
                step(ST[2 * nxt][:], ST[2 * nxt + 1][:], ST[2 * cur][:], ST[2 * cur + 1][:], SRE(c), SIM(c))
                cur = nxt
            b.dma('sp', EXI[0:64, 0:64], ST[2 * cur][:], R=[r_['st']], W=[EXIr])
            b.dma('sp', EXI[64:128, 0:64], ST[2 * cur + 1][:], R=[r_['st']], W=[EXIr])
            b.dma('sp', EXI[:, 64:80], ZL[:], R=[cR['zl']], W=[EXIr])
            if not b.stopped:
                b._deps('pool', [EXIr], [EXOr])
                b.ccn += 1
            if not b.dry and not b.stopped:
                nc.gpsimd.collective_compute("AllGather", ALU.bypass, replica_groups=[list(range(NCORE))],
                                             ins=[EXI.ap().opt()], outs=[EXO.ap().opt()]).then_inc(b.sem['cc'], 1)
            if not b.stopped:
                b._upd(('cc', b.ccn), [EXIr], [EXOr])
            exv = EXO.ap().rearrange("(k q) c -> q k c", q=128)
            b.dma('sp', SF[:], exv, R=[EXOr], W=[r_['sf']])
            b.dma('sp', SFI[:], exv[64:128, :, 0:64], R=[EXOr], W=[r_['sfi']])
            AR, AI = ST[2 * cur][:], ST[2 * cur + 1][:]
            NR, NI = ST[2 * (1 - cur)][:], ST[2 * (1 - cur) + 1][:]
            b.op('dve', lambda g: g.memset(AR, 0.0), R=[r_['st']], W=[r_['st']])
            b.op('dve', lambda g: g.memset(AI, 0.0), R=[r_['st']], W=[r_['st']])
            for k in range(NCORE - 1):
                step(NR, NI, AR, AI, SF[0:64, k, 0:64], SFI[:, k, :], lr=L2K[:, 0, :], li=L2K[:, 1, :],
                     R=[r_['sf'], r_['sfi'], r_['l2k']])
                Rr = [rt, r_['st'], cR['cmk']]
                b.tt('dve', NR, NR, AR, ALU.subtract, R=Rr, W=[r_['st']])
                b.tt('dve', NI, NI, AI, ALU.subtract, R=Rr, W=[r_['st']])
                b.stt('dve', AR, NR, CMK[0:64, k:k + 1], AR, ALU.mult, ALU.add, R=Rr, W=[r_['st']])
                b.stt('dve', AI, NI, CMK[0:64, k:k + 1], AI, ALU.mult, ALU.add, R=Rr, W=[r_['st']])
            b.op('dve', lambda g: g.memset(ZH[:], 0.0), W=[cR['zh']])
            zhf = b.view(ZH, 0, [[1, 16]])
            for k in range(NCORE):
                b.stt('dve', zhf, SF[:, k, 64:80], CMK[:, 8 + k:9 + k], zhf, ALU.mult, ALU.add,
                      R=[r_['sf'], cR['cmk']], W=[cR['zh']])
            rsp = [r_['sp'], r_['spi']]
            b.cp('dve', SP[0:64, :, 0], AR, R=[r_['st']], W=rsp)
            b.cp('dve', SPI[:, :, 0], AI, R=[r_['st']], W=rsp)
            for c in range(64):
                step(NR, NI, AR, AI, SRE(c), SIM(c))
                AR, NR = NR, AR
                AI, NI = NI, AI
                if c < 63:
                    b.cp('dve', SP[0:64, :, c + 1], AR, R=[r_['st']], W=rsp)
                    b.cp('dve', SPI[:, :, c + 1], AI, R=[r_['st']], W=rsp)
            b.dma('sp', ssmp_o[l, 0], AR, R=[r_['st']])
            b.dma('sp', ssmp_o[l, 1], AI, R=[r_['st']])
            b.dma('sp', convp_o[l], ZL[:], R=[cR['zl']])
            b.cp('dve', SP[0:64, :, 64:68], S0[:, 0, :, :], R=[r_['s0']], W=rsp)
            b.cp('dve', SPI[:, :, 64:68], S0[:, 1, :, :], R=[r_['s0']], W=rsp)
            with ExitStack() as sg:
                Q = [sb(sg, "Q%d" % i, [64, 64, NB], F32) for i in range(4)]
                qr = Res()
                lrb = b.view(POWr, 32 * 64, [[1, 64], [0, NB]], npart=64)
                lib = b.view(POWi, 32 * 64, [[1, 64], [0, NB]], npart=64)
                Rq = [qr, r_['s0'], rp, r_['ssb'], r_['imt']]
                b.tt('dve', Q[0][:], S0[:, 0, :, :], lrb, ALU.mult, R=Rq, W=[qr])
                b.tt('dve', Q[1][:], S0[:, 1, :, :], lib, ALU.mult, R=Rq, W=[qr])
                b.tt('dve', Q[2][:], S0[:, 1, :, :], lrb, ALU.mult, R=Rq, W=[qr])
                b.tt('dve', Q[3][:], S0[:, 0, :, :], lib, ALU.mult, R=Rq, W=[qr])
                b.tt('dve', Q[0][:], Q[0][:], Q[1][:], ALU.subtract, R=Rq, W=[qr])
                b.tt('dve', Q[2][:], Q[2][:], Q[3][:], ALU.add, R=Rq, W=[qr])
                b.tt('dve', SN[:, 0, :, :], Q[0][:], SSB[0:64, :, 64:68], ALU.add, R=Rq, W=[r_['sn']])
                b.tt('dve', SN[:, 1, :, :], Q[2][:], IMT[:, :, 64:68], ALU.add, R=Rq, W=[r_['sn']])
                b.dma('sp', ssms_o[l].rearrange("r p g s -> p r g s"), SN[:], R=[r_['sn']])
                b.barrier()
            b.dma('sp', SP[64:128, :, :], SPI[:], R=[r_['spi']], W=[r_['sp']])
            b.barrier()
            stop(3)

            with ExitStack() as sD:
                t1 = sb(sD, "t1d", [128, 8, 128], F32)
                t2 = sb(sD, "t2d", [128, 8, 128], F32)
                CL = sb(sD, "CL", [128, 32, 128], BF16)
                ZT = [sb(sD, "ZT%d" % i, [128, 8, 128], BF16) for i in range(2)]
                YT = sb(sD, "YT", [128, TC], F32)
                ztr = [Res(), Res()]
                t12r, ytr = Res(), Res()
                for i in range(2):
                    b.op('pool', lambda g, i=i: g.memset(ZT[i][:], 0.0), W=[ztr[i]])
                allacc = ACCr
                for ct in range(8):
                    for hf in range(4):
                        cplx_tab(CL[:, hf * 8:(hf + 1) * 8, :], t1, t2, t12r, 1 + hf * 8, 8, ct, CMf, CC2, r_['cm'], False)
                    for r in range(32):
                        z = ZT[r % 2]
                        zd = b.view(z, 0, [[144, 8], [1, 16]])
                        b.cp('pool', zd, b.view(CL, r * 128, [[16, 8], [1, 16]]), R=[t12r], W=[ztr[r % 2]])
                        bkk, ro = r // 7, r % 7
                        o = b.view(PS, bkk * 512 + ro * NCH, [[1, NCH]])
                        for gg in range(8):
                            b.mm(o, z[:, gg, :], SP[:, ct * 8 + gg, :], start=(gg == 0), stop=(gg == 7),
                                 R=[ztr[r % 2], r_['sp']], W=allacc, mark=(gg == 7))
                    for bkk in range(5):
                        nr = min(7, 32 - 7 * bkk)
                        for bq in range(NB):
                            src = b.view(PS, bkk * 512 + 16 * bq, [[NCH, nr], [1, 16]])
                            yi = b.view(YU[ct], BW * bq + 7 * bkk, [[1, nr], [32, 16]])
                            yo = b.view(YT, BW * bq + 7 * bkk, [[1, nr], [32, 16]])
                            b.tt('dve', yo, src, yi, ALU.add, R=allacc + [YUr[ct]], W=[ytr])
                        src = b.view(PS, bkk * 512 + 64, [[NCH, nr], [1, NB]])
                        yi = b.view(YU[ct], 512 + 7 * bkk, [[1, nr], [BW, NB]])
                        yo = b.view(YT, 512 + 7 * bkk, [[1, nr], [BW, NB]])
                        b.tt('dve', yo, src, yi, ALU.add, R=allacc + [YUr[ct]], W=[ytr])
                    b.act(YU[ct][:], YT[:], AF.Gelu_apprx_tanh, R=[ytr], W=[YUr[ct]])
            b.barrier()
            stop(4)

        with ExitStack() as sp_:
            X32 = [sb(sp_, "X32_%d" % i, [128, BW], F32) for i in range(DT)]
            X32r = [Res() for _ in range(DT)]
            XN = [sb(sp_, "XN_%d" % i, [128, BW], BF16) for i in range(DT)]
            XNr = [Res() for _ in range(DT)]
            SQ = [sb(sp_, "SQ%d" % i, [128, BW], BF16) for i in range(2)]
            SQr = [Res(), Res()]
            RS = sb(sp_, "RS", [128, BW], F32)
            RSr = Res()
            GT = [sb(sp_, "GT%d" % i, [128, BW], BF16) for i in range(4)]
            GTr = [Res() for _ in range(4)]
            FT_ = [sb(sp_, "FT%d" % i, [128, BW], F32) for i in range(2)]
            FTr = [Res() for _ in range(2)]
            gti = [0]
            fti = [0]
            P2 = {}

            def pt(name, n, shape, dt):
                if name not in P2:
                    P2[name] = ([sb(sp_, name + str(i), shape, dt) for i in range(n)], [Res() for _ in range(n)])
                return P2[name]

            LNG = sb(sp_, "LNG", [128, 1024], BF16)
            BSB = sb(sp_, "BSB", [128, 8, BW], BF16)
            WTM = sb(sp_, "WTM", [128, 8, 128], BF16)
            for k_ in ('lng', 'bsb', 'wtm'):
                cR[k_] = Res()
            b.dma('pool', LNG[:], lnv_d[l], W=[cR['lng']])
            b.dma('pool', WTM[:], wsgT_d[l], W=[cR['wtm']])
            b.tt('dve', WTM[:], WTM[:], b.view(TRIb, 0, [[0, 8], [1, 128]]), ALU.mult, R=[cR['tri']], W=[cR['wtm']])
            b.dma('pool', BSB[:, :, 0:128], bsgu_d[l], W=[cR['bsb']])
            for tt_ in range(1, 4):
                b.cp('dve', BSB[:, :, tt_ * 128:(tt_ + 1) * 128], BSB[:, :, 0:128], R=[cR['bsb']], W=[cR['bsb']])
            b.cp('dve', BSB[:, :, 512:544], BSB[:, :, 0:32], R=[cR['bsb']], W=[cR['bsb']])

            def gt_next():
                i = gti[0]
                gti[0] = (i + 1) % 4
                return i

            def ft_next():
                i = fti[0]
                fti[0] = (i + 1) % 2
                return i

            def rstd_from_acc(a):
                b.act(hv(RS), accv(a), AF.Sqrt, bias=EPSc, R=[ACCr[a], cR['kc']], W=[RSr])
                b.op('dve', lambda g: g.reciprocal(out=RS[:], in_=RS[:]), R=[RSr], W=[RSr])

            def sumsq_to_acc(src, srcr):
                a = acc_next()
                for ft in range(DT):
                    q = ft % 2
                    b.act(SQ[q][:], src[ft][:], AF.Square, R=[srcr[ft]], W=[SQr[q]])
                    for h in range(2):
                        b.mm(acch(a, h, 0, HW), ONESB[:], SQ[q][:, h * HW:(h + 1) * HW], start=(ft == 0), stop=(ft == DT - 1),
                             R=[SQr[q], cR['ones']], W=[ACCr[a]], mark=(h == 1))
                return a

            def norm_into(dst, dstr, gfun, gres):
                a = sumsq_to_acc(X32, X32r)
                rstd_from_acc(a)
                for ft in range(DT):
                    b.stt('dve', dst[ft][:], X32[ft][:], gfun(ft), RS[:], ALU.mult, ALU.mult,
                          R=[X32r[ft], RSr, gres], W=[dstr[ft]])

            for bk in range(NB):
                for ft in range(DT):
                    rr = [XSr[bk][ft]] if l > 0 else []
                    b.dma('sp', X32[ft][:], xsrc[ft, :, bk * BW:(bk + 1) * BW], R=rr, W=[X32r[ft]])
                norm_into(XN, XNr, G1, cR['vec'])

                if True:
                    MG, MGr = pt("MG", DT, [128, BW], BF16)
                    BI, BIr = pt("BI", 8, [128, BW], BF16)

                    def out_branch(e, wname, A_, Ar_):
                        for pr in range(8):
                            wg, wgr = need('gate', l, 0, 16, e * 2048 + pr * 256)
                            gis = []
                            for t2 in range(2):
                                n = pr * 2 + t2
                                ag = acc_next()
                                proj(ag, wg, wgr, t2 * 128, XN, XNr, 16)
                                gi = gt_next()
                                gis.append(gi)
                                b.act(hv(GT[gi]), accv(ag), AF.Sigmoid, bias=BGATE(e, n), R=[ACCr[ag], cR['vec']], W=[GTr[gi]])
                            wt, wr = need(wname, l, 0, 8, pr * 256)
                            for t2 in range(2):
                                n = pr * 2 + t2
                                gi = gis[t2]
                                ay = acc_next()
                                proj(ay, wt, wr, t2 * 128, A_, Ar_, 8)
                                if e == 2:
                                    b.tt('dve', hv(MG[n]), accv(ay), hv(GT[gi]), ALU.mult, R=[ACCr[ay], GTr[gi]], W=[MGr[n]])
                                else:
                                    fi = ft_next()
                                    b.tt('dve', hv(FT_[fi]), accv(ay), hv(GT[gi]), ALU.mult, R=[ACCr[ay], GTr[gi]], W=[FTr[fi]])
                                    b.tt('pool', MG[n][:], MG[n][:], FT_[fi][:], ALU.add, R=[FTr[fi]], W=[MGr[n]])

                    if True:
                        VGB, vr4 = pt("VGB", 4, [128, 1024], BF16)
                        (VGS,), (vrs,) = pt("VGS", 1, [32, 1024], F32)
                        (VNS,), _u = pt("VNS", 1, [32, 1024], BF16)
                        VGr = vr4 + [vrs]
                        UG, UGr = pt("UG", 8, [128, BW], BF16)
                        AC, ACr = BI, BIr
                        (VSQ,), (vsqr,) = pt("VSQ", 1, [128, 1024], F32)
                        (STT,), (sttr,) = pt("STT", 1, [128, 8], F32)
                        VGt = lambda i: (VGB[i][:, :] if i < 4 else VGS[:, :])
                        for pr in range(4):
                            wt, wr = need('in', l, 0, 16, 4096 + pr * 256)
                            for t2 in range(2):
                                ct = pr * 2 + t2
                                a = acc_next()
                                proj(a, wt, wr, t2 * 128, XN, XNr, 16)
                                b.act(hv(UG[ct]), accv(a), AF.Gelu_apprx_tanh, R=[ACCr[a]], W=[UGr[ct]])
                        for pc in range(4):
                            wt, wr = need('in', l, 0, 16, 5120 + pc * 256)
                            for tt_i in range(5):
                                npt = 128 if tt_i < 4 else 32
                                c0 = tt_i * 128
                                a = acc_next()
                                o = b.view(PS, a * 1024, [[1, 256]], npart=npt)
                                for kt in range(DT):
                                    b.mm(o, XN[kt][:, c0:c0 + npt], wt[:, kt, :], start=(kt == 0), stop=(kt == DT - 1),
                                         R=[wr] + XNr, W=[ACCr[a]], mark=(kt == DT - 1))
                                b.act(VGt(tt_i)[:, pc * 256:(pc + 1) * 256], o, AF.Gelu_apprx_tanh, R=[ACCr[a]], W=[VGr[tt_i]])
                        for tt_i in range(5):
                            npt = 128 if tt_i < 4 else 32
                            vg = VGt(tt_i)
                            vq = VSQ[0:npt, :]
                            s_ = lambda j: STT[0:npt, j:j + 1]
                            Rv = [VGr[tt_i], vsqr, sttr]
                            b.op('dve', lambda g: g.reduce_sum(out=s_(0), in_=vg, axis=AX.X), R=Rv, W=[sttr])
                            b.tt('pool', vq, vg, vg, ALU.mult, R=Rv, W=[vsqr])
                            b.op('dve', lambda g: g.reduce_sum(out=s_(1), in_=vq, axis=AX.X), R=Rv, W=[sttr])
                            b.ts('dve', s_(2), s_(0), 1.0 / 1024, None, ALU.mult, R=Rv, W=[sttr])
                            b.tt('dve', s_(3), s_(2), s_(2), ALU.mult, R=Rv, W=[sttr])
                            b.stt('dve', s_(4), s_(1), 1.0 / 1024, s_(3), ALU.mult, ALU.subtract, R=Rv, W=[sttr])
                            b.act(s_(5), s_(4), AF.Sqrt, bias=KC[0:npt, 0:1], R=Rv + [cR['kc']], W=[sttr])
                            b.op('dve', lambda g: g.reciprocal(out=s_(6), in_=s_(5)), R=Rv, W=[sttr])
                            b.ts('dve', vq, vg, s_(2), s_(6), ALU.subtract, ALU.mult, R=Rv, W=[vsqr])
                            b.tt('dve', vg, vq, LNG[0:npt, :], ALU.mult, R=[vsqr, cR['lng']], W=[VGr[tt_i]])
                            if tt_i == 4:
                                b.cp('act', VNS[:, :], vg, R=[VGr[tt_i]], W=[VGr[tt_i]])
                                b.dma('sp', sguv_o[l, bk], vg, R=[VGr[tt_i]])
                        for h in range(8):
                            a = acc_next()
                            for tt_i in range(4):
                                b.mm(acch(a, tt_i // 2, (tt_i % 2) * 128, (tt_i % 2) * 128 + 128), VGB[tt_i][:, h * 128:(h + 1) * 128],
                                     WTM[:, h, :], start=True, stop=True, R=[VGr[tt_i], cR['wtm']], W=[ACCr[a]], mark=False)
                            b.mm(acch(a, 1, 256, 288), VNS[0:32, h * 128:(h + 1) * 128], WTM[0:32, h, 0:32],
                                 start=True, stop=True, R=[VGr[4], cR['wtm']], W=[ACCr[a]], mark=True)
                            fi = ft_next()
                            b.tt('dve', FT_[fi][:, 0:256], acch(a, 0, 0, 256), BSB[:, h, 0:256], ALU.add, R=[ACCr[a], cR['bsb']], W=[FTr[fi]])
                            b.tt('dve', FT_[fi][:, 256:544], acch(a, 1, 0, 288), BSB[:, h, 256:544], ALU.add, R=[ACCr[a], cR['bsb']], W=[FTr[fi]])
                            b.tt('pool', AC[h][:], FT_[fi][:], UG[h][:], ALU.mult, R=[FTr[fi], UGr[h]], W=[ACr[h]])
                        out_branch(2, 'sgo', AC, ACr)
                        stop(5)

                    if True:
                        AA, AAr = BI, BIr
                        ZP, ZPr = pt("ZP", 2, [128, 548], F32)
                        CV, CVr = pt("CV", 2, [128, 548], F32)
                        (CS,), (csr,) = pt("CS", 1, [128, 8, 2], F32)
                        (SH,), (shr,) = pt("SH", 1, [128, 8, 2], F32)
                        b.dma('sp', SH[:], convc_d[l, bk], W=[shr])
                        for pr in range(4):
                            wc, wcr = need('in', l, 0, 16, 1024 + pr * 256)
                            gis = []
                            for t2 in range(2):
                                a1 = acc_next()
                                proj(a1, wc, wcr, t2 * 128, XN, XNr, 16)
                                gi = gt_next()
                                gis.append(gi)
                                b.cp('act', hv(GT[gi]), accv(a1), R=[ACCr[a1]], W=[GTr[gi]])
                            whh, whr = need('in', l, 0, 16, 2048 + pr * 256)
                            for t2 in range(2):
                                ct = pr * 2 + t2
                                zi = ct % 2
                                zp, cv = ZP[zi], CV[zi]
                                gi = gis[t2]
                                a2 = acc_next()
                                proj(a2, whh, whr, t2 * 128, XN, XNr, 16)
                                for (hh, lo, hi, c0, c1) in pieces(0, 512):
                                    b.tt('dve', zp[:, 2 + c0:2 + c1], acch(a2, hh, lo, hi), GT[gi][:, c0:c1], ALU.mult,
                                         R=[ACCr[a2], GTr[gi]], W=[ZPr[zi]])
                                b.tt('dve', zp[:, 516:548], acch(a2, 1, 512 - HW, BW - HW), GT[gi][:, 512:544], ALU.mult,
                                     R=[ACCr[a2], GTr[gi]], W=[ZPr[zi]])
                                b.cp('pool', zp[:, 0:2], ZH[:, ct, :], R=[cR['zh']], W=[ZPr[zi]])
                                b.cp('pool', zp[:, 514:516], SH[:, ct, :], R=[shr], W=[ZPr[zi]])
                                b.cp('pool', ZH[:, ct, :], zp[:, 512:514], R=[ZPr[zi]], W=[cR['zh']])
                                b.cp('pool', CS[:, ct, :], zp[:, 546:548], R=[ZPr[zi]], W=[csr])
                                b.ts('dve', cv[:, 0:546], zp[:, 2:548], CW(2, ct), None, ALU.mult, R=[ZPr[zi], cR['vec']], W=[CVr[zi]])
                                for kk in (1, 0):
                                    b.stt('dve', cv[:, 0:546], zp[:, kk:kk + 546], CW(kk, ct), cv[:, 0:546], ALU.mult, ALU.add,
                                          R=[ZPr[zi], cR['vec']], W=[CVr[zi]])
                            wb_, wbr = need('in', l, 0, 16, pr * 256)
                            for t2 in range(2):
                                ct = pr * 2 + t2
                                zi = ct % 2
                                cv = CV[zi]
                                a3 = acc_next()
                                proj(a3, wb_, wbr, t2 * 128, XN, XNr, 16)
                                for (hh, lo, hi, c0, c1) in pieces(0, 512):
                                    b.tt('dve', AA[ct][:, c0:c1], acch(a3, hh, lo, hi), cv[:, c0:c1], ALU.mult,
                                         R=[ACCr[a3], CVr[zi]], W=[AAr[ct]])
                                b.tt('dve', AA[ct][:, 512:544], acch(a3, 1, 512 - HW, BW - HW), cv[:, 514:546], ALU.mult,
                                     R=[ACCr[a3], CVr[zi]], W=[AAr[ct]])
                        b.dma('sp', convs_o[l, bk], CS[:], R=[csr])
                        out_branch(0, 'cvo', AA, AAr)
                        stop(6)

                    if True:
                        AB, ABr = BI, BIr
                        YS = [YU[ct][:, bk * BW:(bk + 1) * BW] for ct in range(8)]
                        for pr in range(4):
                            wt, wr = need('glu', l, 0, 8, pr * 256)
                            for t2 in range(2):
                                ct = pr * 2 + t2
                                a = acc_next()
                                proj(a, wt, wr, t2 * 128, YS, YUr, 8)
                                gi = gt_next()
                                b.act(hv(GT[gi]), accv(a), AF.Sigmoid, bias=BGLU(ct), R=[ACCr[a], cR['vec']], W=[GTr[gi]])
                                b.tt('pool', AB[ct][:], GT[gi][:], YS[ct], ALU.mult, R=[GTr[gi], YUr[ct]], W=[ABr[ct]])
                        out_branch(1, 'sso', AB, ABr)
                        stop(7)

                    for pr in range(8):
                        wt, wr = need('o', l, 0, 16, pr * 256)
                        for t2 in range(2):
                            n = pr * 2 + t2
                            a = acc_next()
                            proj(a, wt, wr, t2 * 128, MG, MGr, 16)
                            b.tt('dve', hv(X32[n]), accv(a), hv(X32[n]), ALU.add, R=[ACCr[a]], W=[X32r[n]])

                norm_into(XN, XNr, G2, cR['vec'])
                if True:
                    NF = 22
                    A_ = MG + BI[0:NF - DT]
                    Ar_ = MGr + BIr[0:NF - DT]
                    for hf in range(2):
                        for pr in range(NF // 2):
                            f0 = hf * NF * 128 + pr * 256
                            wg, wgr = need('fi', l, 0, 16, f0)
                            gis = []
                            for t2 in range(2):
                                ag = acc_next()
                                proj(ag, wg, wgr, t2 * 128, XN, XNr, 16)
                                gi = gt_next()
                                gis.append(gi)
                                b.act(hv(GT[gi]), accv(ag), AF.Silu, R=[ACCr[ag]], W=[GTr[gi]])
                            wu, wur = need('fi', l, 0, 16, DFF + f0)
                            for t2 in range(2):
                                f = pr * 2 + t2
                                gi = gis[t2]
                                au = acc_next()
                                proj(au, wu, wur, t2 * 128, XN, XNr, 16)
                                b.tt('dve', hv(A_[f]), accv(au), hv(GT[gi]), ALU.mult, R=[ACCr[au], GTr[gi]], W=[Ar_[f]])
                        for pr in range(8):
                            w1, w1r = need('fo', l, hf * NF, 16, pr * 256)
                            aa = [acc_next(), acc_next()]
                            for t2 in range(2):
                                proj(aa[t2], w1, w1r, t2 * 128, A_[0:16], Ar_[0:16], 16, first=True, last=False)
                            w2, w2r = need('fo', l, hf * NF + 16, NF - 16, pr * 256)
                            for t2 in range(2):
                                n = pr * 2 + t2
                                proj(aa[t2], w2, w2r, t2 * 128, A_[16:NF], Ar_[16:NF], NF - 16, first=False, last=True)
                                b.tt('dve', hv(X32[n]), accv(aa[t2]), hv(X32[n]), ALU.add, R=[ACCr[aa[t2]]], W=[X32r[n]])

                if not last_layer:
                    for ft in range(DT):
                        b.dma('sp', XS[ft, :, bk * BW:(bk + 1) * BW], X32[ft][:], R=[X32r[ft]], W=[XSr[bk][ft]])
                else:
                    a = sumsq_to_acc(X32, X32r)
                    rstd_from_acc(a)
                    for ft in range(DT):
                        b.stt('dve', X32[ft][:], X32[ft][:], GF[:, ft:ft + 1], RS[:], ALU.mult, ALU.mult,
                              R=[RSr, cR['gf']], W=[X32r[ft]])
                        b.dma('sp', yT[ft, :, bk * BW:(bk + 1) * BW], X32[ft][:], R=[X32r[ft]])
            b.barrier()

    body()
    b.stopped = False
    b.barrier()
    for k, v in b.semval.items():
        b._wait('sp', k, v)
    es.close()
    return b


def build_nc():
    plan = []
    nc0 = bass.Bass("TRN2", target_bir_lowering=False)
    _build(nc0, True, plan)
    nc = bass.Bass("TRN2", target_bir_lowering=False)
    _build(nc, False, plan)
    return nc


_PROBE_ONLY = False


def _consts():
    c = np.zeros((128, 4, 128), np.float32)
    s = np.arange(128)
    c[:, 0, :] = (s[:, None] <= s[None, :]).astype(np.float32)
    c[:, 1, :] = ((s[:, None] // 16) == (s[None, :] // 16)).astype(np.float32)
    c[:, 2, :] = np.eye(128, dtype=np.float32)
    for gg in range(8):
        c[:, 3, gg] = (s // 16 == gg)
    c[:, 3, 8] = np.where(s < 64, 1.0, -1.0)
    return c


def kernel(x_prompt, x_sample, cache_conv, state_ssm_re, state_ssm_im, norm_mix_g, w_in, conv_w,
           w_conv_out, ssm_lam_re, ssm_lam_im, ssm_log_dt, ssm_b_re, ssm_b_im, ssm_c_re, ssm_c_im,
           ssm_d, w_glu, b_glu, w_ssm_out, ln_v_g, w_sgu_s, b_sgu_s, w_sgu_out, w_gate, b_gate, w_o,
           norm_ffn_g, w_ffn_in, w_ffn_out, norm_final_g):
    f = lambda a: np.ascontiguousarray(np.asarray(a, dtype=np.float32))
    x_prompt, x_sample = f(x_prompt), f(x_sample)
    L = DEPTH
    pl = lambda v: f(np.asarray(v, np.float32).reshape(L, -1, 128).transpose(0, 2, 1))
    vecs = np.zeros((L, 128, NV), np.float32)
    vecs[:, :, V_G1:V_G1 + 16] = pl(norm_mix_g)
    vecs[:, :, V_G2:V_G2 + 16] = pl(norm_ffn_g)
    cw = np.asarray(conv_w, np.float32).reshape(L, 3, 8, 128).transpose(0, 3, 1, 2)
    vecs[:, :, V_CW:V_CW + 24] = cw.reshape(L, 128, 24)
    vecs[:, :, V_D:V_D + 8] = pl(ssm_d)
    vecs[:, :, V_BGLU:V_BGLU + 8] = pl(b_glu)
    bg = np.asarray(b_gate, np.float32).reshape(L, 3, 16, 128).transpose(0, 3, 1, 2)
    vecs[:, :, V_BGATE:V_BGATE + 48] = bg.reshape(L, 128, 48)
    gF = f(np.asarray(norm_final_g, np.float32).reshape(16, 128).T)
    lnv = f(np.broadcast_to(np.asarray(ln_v_g, np.float32)[:, None, :], (L, 128, 1024)))
    bsgu = f(np.broadcast_to(np.asarray(b_sgu_s, np.float32)[:, None, :, :], (L, 128, 8, 128)))
    wsgT = f(np.asarray(w_sgu_s, np.float32).transpose(0, 3, 1, 2))
    lre = np.asarray(ssm_lam_re, np.float32).transpose(0, 2, 1)
    lim = np.asarray(ssm_lam_im, np.float32).transpose(0, 2, 1)
    lam = np.stack([lre, lim], axis=2)
    lam = f(np.concatenate([lam, lam], axis=1))
    ldt = f(np.broadcast_to(np.asarray(ssm_log_dt, np.float32)[:, None, :], (L, 128, 64)))
    br = np.asarray(ssm_b_re, np.float32).transpose(0, 2, 1, 3).reshape(L, 64, 1024)
    bi = np.asarray(ssm_b_im, np.float32).transpose(0, 2, 1, 3).reshape(L, 64, 1024)
    cr = np.asarray(ssm_c_re, np.float32).transpose(0, 3, 1, 2).reshape(L, 64, 1024)
    ci = np.asarray(ssm_c_im, np.float32).transpose(0, 3, 1, 2).reshape(L, 64, 1024)
    B1 = f(np.concatenate([br, bi], axis=1))
    B2 = f(np.concatenate([bi, br], axis=1))
    C1 = f(np.concatenate([cr, ci], axis=1))
    C2 = f(np.concatenate([ci, cr], axis=1))
    shared = {
        "vecs": vecs, "gF": gF, "lnv": lnv, "bsgu": bsgu, "wsgT": wsgT, "cst": _consts(),
        "lam": lam, "ldt": ldt, "B1": B1, "B2": B2, "C1": C1, "C2": C2,
    }
    wsrc = {"w_in": w_in, "w_gate": np.asarray(w_gate).reshape(L, D, 6144), "w_conv_out": w_conv_out, "w_glu": w_glu,
            "w_ssm_out": w_ssm_out, "w_sgu_out": w_sgu_out, "w_o": w_o, "w_ffn_in": w_ffn_in, "w_ffn_out": w_ffn_out}
    for nm, arr in wsrc.items():
        arr = np.asarray(arr)
        for l_ in range(RUN_DEPTH):
            shared["%s_%d" % (nm, l_)] = f(arr[l_]) if (KSTOP > 4 or nm == 'w_in') else np.zeros((128, 256), np.float32)
    in_maps = []
    xpT = x_prompt[0].T
    for k in range(NCORE):
        xc = np.empty((D, TC), np.float32)
        for bk in range(NB):
            t0 = k * 2048 + bk * 512
            xc[:, bk * BW:bk * BW + 512] = xpT[:, t0:t0 + 512]
            xc[:, bk * BW + 512:(bk + 1) * BW] = x_sample[4 * k + bk].T
        m = dict(shared)
        m["xT"] = f(xc.reshape(DT, 128, TC))
        cc = np.asarray(cache_conv, np.float32)[:, 4 * k:4 * k + 4]
        m["convc"] = f(cc.reshape(L, NB, 2, 8, 128).transpose(0, 1, 4, 3, 2))
        sr = np.asarray(state_ssm_re, np.float32)[:, 4 * k:4 * k + 4]
        si = np.asarray(state_ssm_im, np.float32)[:, 4 * k:4 * k + 4]
        st = np.stack([sr, si], axis=1)
        m["st0"] = f(st.transpose(0, 1, 4, 3, 2))
        cm = np.zeros((128, 16), np.float32)
        cm[:, 0:8] = (np.arange(8) < k).astype(np.float32)[None, :]
        if k > 0:
            cm[:, 8 + k - 1] = 1.0
        m["cmask"] = cm
        in_maps.append(m)

    if _PROBE_ONLY:
        return in_maps
    nc = build_nc()
    res = run_bass_kernel_spmd(nc, in_maps, core_ids=list(range(NCORE)))
    R = res.results

    y_prompt = np.empty((1, 16384, D), np.float32)
    y_sample = np.empty((32, 32, D), np.float32)
    conv_s = np.empty((L, 32, 2, 1024), np.float32)
    re_s = np.empty((L, 32, 64, 64), np.float32)
    im_s = np.empty((L, 32, 64, 64), np.float32)
    v_s = np.empty((L, 32, 32, 1024), np.float32)
    for k in range(NCORE):
        yc = R[k]["yT"].reshape(D, TC)
        for bk in range(NB):
            t0 = k * 2048 + bk * 512
            y_prompt[0, t0:t0 + 512] = yc[:, bk * BW:bk * BW + 512].T
            y_sample[4 * k + bk] = yc[:, bk * BW + 512:(bk + 1) * BW].T
        cs = R[k]["convs"]
        conv_s[:, 4 * k:4 * k + 4] = cs.transpose(0, 1, 4, 3, 2).reshape(L, NB, 2, 1024)
        ss_ = R[k]["ssms"]
        re_s[:, 4 * k:4 * k + 4] = ss_[:, 0].transpose(0, 3, 2, 1)
        im_s[:, 4 * k:4 * k + 4] = ss_[:, 1].transpose(0, 3, 2, 1)
        v_s[:, 4 * k:4 * k + 4] = R[k]["sguv"]
    cp = R[NCORE - 1]["convp"].reshape(L, 128, 8, 2)
    conv_p = np.ascontiguousarray(cp.transpose(0, 3, 2, 1).reshape(L, 1, 2, 1024))
    sp_ = R[NCORE - 1]["ssmp"]
    re_p = np.ascontiguousarray(sp_[:, 0].transpose(0, 2, 1)[:, None])
    im_p = np.ascontiguousarray(sp_[:, 1].transpose(0, 2, 1)[:, None])
    return (y_prompt, y_sample, conv_p, re_p, im_p, conv_s, re_s, im_s, v_s)
```

```python
import math
from contextlib import ExitStack

import numpy as np
import concourse.bass as bass
import concourse.mybir as mybir
from concourse.bass_utils import run_bass_kernel_spmd

F32 = mybir.dt.float32
BF16 = mybir.dt.bfloat16
AF = mybir.ActivationFunctionType
ALU = mybir.AluOpType
AX = mybir.AxisListType

import os
NCORE = 8
RUN_DEPTH = int(os.environ.get('KDEPTH', '4'))
KSTOP = float(os.environ.get('KSTOP', '99'))


class StopBuild(Exception):
    pass

D = 2048
DT = 16
DEPTH = 4
TC = 2176
NB = 4
BW = 544
HW = 272
NCH = 68
DFF = 5632
EPS = 1e-6
NW = 3
ND = 8
ENGS = ['pe', 'act', 'dve', 'pool', 'sp']

V_G1, V_G2, V_CW, V_D, V_BGLU, V_BGATE, NV = 0, 16, 32, 56, 64, 72, 120


class Res:
    __slots__ = ('w', 'r')

    def __init__(self):
        self.w = None
        self.r = {}


def pieces(c0, c1):
    out = []
    if c0 < HW:
        e = min(c1, HW)
        out.append((0, c0, e, c0, e))
    if c1 > HW:
        s = max(c0, HW)
        out.append((1, s - HW, c1 - HW, s, c1))
    return out


class Bld:
    def __init__(self, nc, dry, plan):
        self.nc = nc
        self.dry = dry
        self.plan = plan
        self.pidx = 0
        self.issued = 0
        self.eng = {'pe': nc.tensor, 'act': nc.scalar, 'dve': nc.vector, 'pool': nc.gpsimd, 'sp': nc.sync}
        self.cnt = {e: 0 for e in ENGS}
        self.seen = {e: {} for e in ENGS}
        self.sem = {}
        self.semval = {}
        self.dq = {'sp': 0, 'pool': 0}
        self.es = ExitStack()
        for e in ENGS:
            self.sem[e] = self.es.enter_context(nc.semaphore('s_' + e))
        for q in ('sp', 'pool'):
            for i in range(ND):
                self.sem['d%s%d' % (q, i)] = self.es.enter_context(nc.semaphore('d%s%d' % (q, i)))
        self.sem['cc'] = self.es.enter_context(nc.semaphore('cc'))
        self.ccn = 0
        self.acc_i = 0
        self.stopped = False

    def _wait(self, e, key, val):
        if self.stopped:
            return
        if self.seen[e].get(key, 0) >= val:
            return
        self.seen[e][key] = val
        if not self.dry:
            self.eng[e].wait_ge(self.sem[key], val)

    def _deps(self, e, R, W):
        for r in R:
            if r.w is not None:
                k, v = r.w
                if not (e == 'pe' and k == 'pe'):
                    self._wait(e, k, v)
        for w in W:
            if w.w is not None:
                k, v = w.w
                if not (e == 'pe' and k == 'pe'):
                    self._wait(e, k, v)
            for k, v in w.r.items():
                if not (e == 'pe' and k == 'pe'):
                    self._wait(e, k, v)

    def _upd(self, evt, R, W):
        k, v = evt
        self.semval[k] = max(self.semval.get(k, 0), v)
        for r in R:
            if r.r.get(k, 0) < v:
                r.r[k] = v
        for w in W:
            w.w = evt
            w.r = {}

    def op(self, e, fn, R=(), W=(), mark=True):
        if self.stopped:
            return
        self._deps(e, R, W)
        if mark:
            self.cnt[e] += 1
            evt = (e, self.cnt[e])
        else:
            evt = (e, self.cnt[e] + 1)
        if not self.dry:
            ins = fn(self.eng[e])
            if mark:
                ins.then_inc(self.sem[e], 1)
        self._upd(evt, R, W)

    def dma(self, q, out, in_, R=(), W=()):
        if self.stopped:
            return
        i = self.dq[q]
        self.dq[q] += 1
        key = 'd%s%d' % (q, i % ND)
        val = 16 * (i // ND + 1)
        if i >= ND:
            self._wait(q, key, val - 16)
        self._deps(q, R, W)
        if not self.dry:
            self.eng[q].dma_start(out=out, in_=in_).then_inc(self.sem[key], 16)
        self._upd((key, val), R, W)

    def barrier(self):
        for e in ENGS:
            for k, v in self.semval.items():
                if not (e == 'pe' and k == 'pe'):
                    self._wait(e, k, v)

    def mm(self, out, lhsT, rhs, start, stop, R=(), W=(), mark=False):
        self.op('pe', lambda pe: pe.matmul(out, lhsT=lhsT, rhs=rhs, start=start, stop=stop), R=R, W=W, mark=mark)

    def tt(self, e, out, in0, in1, op, R=(), W=()):
        self.op(e, lambda g: g.tensor_tensor(out=out, in0=in0, in1=in1, op=op), R=R, W=W)

    def ts(self, e, out, in0, s1, s2, op0, op1=None, R=(), W=()):
        if op1 is None:
            self.op(e, lambda g: g.tensor_scalar(out=out, in0=in0, scalar1=s1, scalar2=None, op0=op0), R=R, W=W)
        else:
            self.op(e, lambda g: g.tensor_scalar(out=out, in0=in0, scalar1=s1, scalar2=s2, op0=op0, op1=op1), R=R, W=W)

    def stt(self, e, out, in0, scalar, in1, op0, op1, R=(), W=()):
        self.op(e, lambda g: g.scalar_tensor_tensor(out=out, in0=in0, scalar=scalar, in1=in1, op0=op0, op1=op1), R=R, W=W)

    def act(self, out, in_, func, bias=None, scale=None, R=(), W=()):
        kw = {}
        if bias is not None:
            kw['bias'] = bias
        if scale is not None:
            kw['scale'] = scale
        self.op('act', lambda g: g.activation(out=out, in_=in_, func=func, **kw), R=R, W=W)

    def cp(self, e, out, in_, R=(), W=()):
        if e == 'act':
            self.op('act', lambda g: g.copy(out=out, in_=in_), R=R, W=W)
        else:
            self.op(e, lambda g: g.tensor_copy(out=out, in_=in_), R=R, W=W)

    def sb(self, scope, name, shape, dt):
        self.nalloc = getattr(self, 'nalloc', 0) + 1
        return scope.enter_context(self.nc.sbuf_tensor("%s_%d" % (name, self.nalloc), shape, dt))

    def view(self, t, off, dims, p0=0, npart=128):
        a = t[:] if not isinstance(t, bass.AP) else t
        pstep = a.ap[0][0]
        return bass.AP(a.tensor, a.offset + p0 * pstep + off, [[pstep, npart]] + [list(d) for d in dims])


def _build(nc, dry, plan):
    b = Bld(nc, dry, plan)
    es = b.es

    def din(name, shape):
        return nc.dram_tensor(name, list(shape), F32, kind="ExternalInput").ap()

    def dout(name, shape):
        return nc.dram_tensor(name, list(shape), F32, kind="ExternalOutput").ap()

    xT = din("xT", [DT, 128, TC])
    wshapes = {'in': ("w_in", D, 6144), 'gate': ("w_gate", D, 6144), 'cvo': ("w_conv_out", 1024, D),
               'glu': ("w_glu", 1024, 1024), 'sso': ("w_ssm_out", 1024, D), 'sgo': ("w_sgu_out", 1024, D),
               'o': ("w_o", D, D), 'fi': ("w_ffn_in", D, 2 * DFF), 'fo': ("w_ffn_out", DFF, D)}
    if KSTOP <= 4:
        wshapes = {k: (v if k == 'in' else (v[0], 128, 256)) for k, v in wshapes.items()}
    Wd = {k: [din("%s_%d" % (nm, l_), [kk, nn]) for l_ in range(RUN_DEPTH)] for k, (nm, kk, nn) in wshapes.items()}
    vecs_d = din("vecs", [DEPTH, 128, NV])
    gF_d = din("gF", [128, DT])
    lnv_d = din("lnv", [DEPTH, 128, 1024])
    bsgu_d = din("bsgu", [DEPTH, 128, 8, 128])
    wsgT_d = din("wsgT", [DEPTH, 128, 8, 128])
    cst_d = din("cst", [128, 4, 128])
    lam_d = din("lam", [DEPTH, 128, 2, 64])
    ldt_d = din("ldt", [DEPTH, 128, 64])
    B1_d = din("B1", [DEPTH, 128, 1024])
    B2_d = din("B2", [DEPTH, 128, 1024])
    C1_d = din("C1", [DEPTH, 128, 1024])
    C2_d = din("C2", [DEPTH, 128, 1024])
    convc_d = din("convc", [DEPTH, NB, 128, 8, 2])
    st0_d = din("st0", [DEPTH, 2, 64, 64, NB])
    cmask_d = din("cmask", [128, 16])

    yT = dout("yT", [DT, 128, TC])
    convp_o = dout("convp", [DEPTH, 128, 16])
    convs_o = dout("convs", [DEPTH, NB, 128, 8, 2])
    ssmp_o = dout("ssmp", [DEPTH, 2, 64, 64])
    ssms_o = dout("ssms", [DEPTH, 2, 64, 64, NB])
    sguv_o = dout("sguv", [DEPTH, NB, 32, 1024])

    XS = nc.dram_tensor("xs", [DT, 128, TC], F32).ap()
    EXI = nc.dram_tensor("exi", [128, 80], F32)
    EXO = nc.dram_tensor("exo", [NCORE * 128, 80], F32)
    XSr = [[Res() for _ in range(DT)] for _ in range(NB)]
    EXIr, EXOr = Res(), Res()

    sb = b.sb
    PS = es.enter_context(nc.psum_tensor("PS", [128, 8, 512], F32))
    WS = [sb(es, "WS%d" % i, [128, 16, 256], BF16) for i in range(NW)]
    WR = [Res() for _ in range(NW)]
    YU = [sb(es, "YU%d" % c, [128, TC], BF16) for c in range(8)]
    YUr = [Res() for _ in range(8)]
    CST = sb(es, "CST", [128, 4, 128], F32)
    TRIb = sb(es, "TRIb", [128, 128], BF16)
    IDCM = sb(es, "IDCM", [128, 8, 256], BF16)
    ONESB = sb(es, "ONESB", [128, 128], BF16)
    BD4 = sb(es, "BD4", [128, 4, 128], F32)
    GF = sb(es, "GF", [128, DT], F32)
    CMK = sb(es, "CMK", [128, 16], F32)
    VEC = sb(es, "VEC", [128, NV], F32)
    ZH = sb(es, "ZH", [128, 8, 2], F32)
    ZL = sb(es, "ZL", [128, 16], F32)
    KC = sb(es, "KC", [128, 8], F32)
    cR = {k: Res() for k in ('cst', 'tri', 'idcm', 'ones', 'gf', 'cmk', 'vec', 'lng', 'bsb', 'wtm', 'zh', 'zl', 'kc')}
    ACCr = [Res() for _ in range(4)]

    def acc_next():
        i = b.acc_i
        b.acc_i = (i + 1) % 4
        return i

    def accv(s):
        return b.view(PS, s * 1024, [[512, 2], [1, HW]])

    def acch(s, h, lo, hi):
        return b.view(PS, s * 1024 + h * 512 + lo, [[1, hi - lo]])

    def accflat(s, n):
        return b.view(PS, s * 1024, [[1, n]])

    def hv(t, off=0):
        return b.view(t, off, [[HW, 2], [1, HW]])

    b.dma('sp', CST[:], cst_d[:], W=[cR['cst']])
    b.dma('sp', GF[:], gF_d[:], W=[cR['gf']])
    b.dma('sp', CMK[:], cmask_d[:], W=[cR['cmk']])
    b.cp('dve', TRIb[:], CST[:, 0, :], R=[cR['cst']], W=[cR['tri']])
    b.op('dve', lambda g: g.memset(ONESB[:], 1.0 / D), W=[cR['ones']])
    b.op('dve', lambda g: g.memset(KC[:, 0:1], EPS), W=[cR['kc']])
    b.op('dve', lambda g: g.memset(KC[:, 1:2], math.pi / 2), W=[cR['kc']])
    b.op('dve', lambda g: g.memset(KC[:, 2:3], 0.0), W=[cR['kc']])
    b.op('dve', lambda g: g.memset(KC[:, 3:4], 1.0), W=[cR['kc']])
    BDm = CST[:, 1, :]
    IDf = CST[:, 2, :]
    MCOL = lambda gg: CST[:, 3, gg:gg + 1]
    SGN = CST[:, 3, 8:9]
    EPSc = KC[:, 0:1]
    HPIc = KC[:, 1:2]
    for c in range(8):
        b.cp('dve', IDCM[:, c, 0:128], IDf, R=[cR['cst']], W=[cR['idcm']])
    for c in range(4):
        b.cp('dve', BD4[:, c, :], BDm, R=[cR['cst']], W=[cR['idcm']])

    def issue(j):
        name, l, kt0, nkt, n0, ncols = b.plan[j]
        s = j % NW
        src = Wd[name][l][kt0 * 128:(kt0 + nkt) * 128, n0:n0 + ncols].rearrange("(kt p) n -> p kt n", p=128)
        b.dma('pool', WS[s][:, 0:nkt, 0:ncols], src, W=[WR[s]])

    def need(name, l, kt0, nkt, n0, ncols=256):
        desc = (name, l, kt0, nkt, n0, ncols)
        if b.stopped:
            return WS[0], WR[0]
        if b.dry:
            b.plan.append(desc)
            return WS[0], WR[0]
        assert b.plan[b.pidx] == desc, (b.plan[b.pidx], desc)
        while b.issued < min(len(b.plan), b.pidx + NW):
            issue(b.issued)
            b.issued += 1
        s = b.pidx % NW
        b.pidx += 1
        return WS[s], WR[s]

    def proj(a, wt, wr, col, xs, xr, nkt, wk0=0, first=True, last=True, split=HW):
        for kt in range(nkt):
            for h in range(2):
                fin = last and kt == nkt - 1 and h == 1
                cs, ce = (0, split) if h == 0 else (split, BW)
                b.mm(acch(a, h, 0, ce - cs), wt[:, wk0 + kt, col:col + 128], xs[kt][:, cs:ce],
                     start=(first and kt == 0), stop=(last and kt == nkt - 1),
                     R=[wr, xr[kt]], W=[ACCr[a]], mark=fin)

    def stop(n):
        if KSTOP <= n and not b.stopped:
            b.barrier()
            b.stopped = True

    def body():
      for l in range(RUN_DEPTH):
        last_layer = (l == RUN_DEPTH - 1)
        b.dma('sp', VEC[:], vecs_d[l], W=[cR['vec']])
        G1 = lambda ft: VEC[:, V_G1 + ft:V_G1 + ft + 1]
        G2 = lambda ft: VEC[:, V_G2 + ft:V_G2 + ft + 1]
        CW = lambda k, ct: VEC[:, V_CW + k * 8 + ct:V_CW + k * 8 + ct + 1]
        DV = lambda ct: VEC[:, V_D + ct:V_D + ct + 1]
        BGLU = lambda ct: VEC[:, V_BGLU + ct:V_BGLU + ct + 1]
        BGATE = lambda e, ft: VEC[:, V_BGATE + e * 16 + ft:V_BGATE + e * 16 + ft + 1]

        xsrc = xT if l == 0 else XS

        with ExitStack() as sp_:
            X32 = [sb(sp_, "X32_%d" % i, [128, BW], F32) for i in range(DT)]
            X32r = [Res() for _ in range(DT)]
            XN = [sb(sp_, "XN_%d" % i, [128, BW], BF16) for i in range(DT)]
            XNr = [Res() for _ in range(DT)]
            SQ = [sb(sp_, "SQ%d" % i, [128, BW], BF16) for i in range(2)]
            SQr = [Res(), Res()]
            RS = sb(sp_, "RS", [128, BW], F32)
            RSr = Res()

            def rstd_from_acc(a):
                b.act(hv(RS), accv(a), AF.Sqrt, bias=EPSc, R=[ACCr[a], cR['kc']], W=[RSr])
                b.op('dve', lambda g: g.reciprocal(out=RS[:], in_=RS[:]), R=[RSr], W=[RSr])

            def sumsq_to_acc(src, srcr):
                a = acc_next()
                for ft in range(DT):
                    q = ft % 2
                    b.act(SQ[q][:], src[ft][:], AF.Square, R=[srcr[ft]], W=[SQr[q]])
                    for h in range(2):
                        b.mm(acch(a, h, 0, HW), ONESB[:], SQ[q][:, h * HW:(h + 1) * HW], start=(ft == 0), stop=(ft == DT - 1),
                             R=[SQr[q], cR['ones']], W=[ACCr[a]], mark=(h == 1))
                return a

            def load_norm(bk, gfun):
                for ft in range(DT):
                    rr = [XSr[bk][ft]] if l > 0 else []
                    b.dma('sp', X32[ft][:], xsrc[ft, :, bk * BW:(bk + 1) * BW], R=rr, W=[X32r[ft]])
                a = sumsq_to_acc(X32, X32r)
                rstd_from_acc(a)
                for ft in range(DT):
                    b.stt('dve', XN[ft][:], X32[ft][:], gfun(ft), RS[:], ALU.mult, ALU.mult,
                          R=[X32r[ft], RSr, cR['vec']], W=[XNr[ft]])

            for bk in range(NB):
                load_norm(bk, G1)
                for pr in range(4):
                    wt, wr = need('in', l, 0, 16, 3072 + pr * 256)
                    for t2 in range(2):
                        ct = pr * 2 + t2
                        a = acc_next()
                        proj(a, wt, wr, t2 * 128, XN, XNr, 16, split=256)
                        for h in range(2):
                            src = b.view(PS, a * 1024 + h * 512, [[32, 8], [1, 32]])
                            dst = b.view(YU[ct], 16 * bk + 8 * h, [[1, 8], [NCH, 32]])
                            b.cp('act' if h == 0 else 'dve', dst, src, R=[ACCr[a]], W=[YUr[ct]])
                        src = acch(a, 1, 256, 288)
                        dst = b.view(YU[ct], 64 + bk, [[NCH, 32]])
                        b.cp('dve', dst, src, R=[ACCr[a]], W=[YUr[ct]])
                if bk == NB - 1:
                    a = acc_next()
                    xl = [XN[kt][:, 510:512] for kt in range(DT)]
                    for which in range(2):
                        for pr in range(4):
                            wt, wr = need('in', l, 0, 16, 1024 * (1 + which) + pr * 256)
                            for t2 in range(2):
                                ct = pr * 2 + t2
                                o = b.view(PS, a * 1024 + which * 16 + ct * 2, [[1, 2]])
                                for kt in range(DT):
                                    b.mm(o, wt[:, kt, t2 * 128:t2 * 128 + 128], xl[kt], start=(kt == 0), stop=(kt == DT - 1),
                                         R=[wr, XNr[kt]], W=[ACCr[a]], mark=(kt == DT - 1))
                    with ExitStack() as sz:
                        CGt = sb(sz, "CGt", [128, 16], BF16)
                        cgr = Res()
                        b.cp('act', CGt[:], accflat(a, 16), R=[ACCr[a]], W=[cgr])
                        b.tt('dve', ZL[:], b.view(PS, a * 1024 + 16, [[1, 16]]), CGt[:], ALU.mult, R=[ACCr[a], cgr], W=[cR['zl']])
                        b.barrier()
            b.barrier()
            stop(1)
        with ExitStack() as ss:
            LAM = sb(ss, "LAM", [128, 2, 64], F32)
            LDT = sb(ss, "LDT", [128, 64], F32)
            POWr = sb(ss, "POWr", [128, 33, 64], F32)
            POWi = sb(ss, "POWi", [128, 33, 64], F32)
            BB1 = sb(ss, "BB1", [128, 1024], F32)
            BB2 = sb(ss, "BB2", [128, 1024], F32)
            CMf = sb(ss, "CMf", [128, 1024], F32)
            CC2 = sb(ss, "CC2", [128, 1024], F32)
            SSB = sb(ss, "SSB", [128, 64, NCH], F32)
            IMT = sb(ss, "IMT", [64, 64, NCH], F32)
            SP = sb(ss, "SP", [128, 64, NCH], BF16)
            SPI = sb(ss, "SPI", [64, 64, NCH], BF16)
            T = [sb(ss, "T%d" % i, [128, 64], F32) for i in range(12)]
            ST = [sb(ss, "ST%d" % i, [64, 64], F32) for i in range(4)]
            SF = sb(ss, "SF", [128, 8, 80], F32)
            SFI = sb(ss, "SFI", [64, 8, 64], F32)
            L2K = sb(ss, "L2K", [64, 2, 64], F32)
            S0 = sb(ss, "S0", [64, 2, 64, NB], F32)
            SN = sb(ss, "SN", [64, 2, 64, NB], F32)
            r_ = {k: Res() for k in ('lam', 'pow', 'bb', 'cm', 'ssb', 'imt', 'sp', 'spi', 't', 'st', 'sf', 'sfi', 'l2k', 's0', 'sn')}
            rt = r_['t']

            b.dma('sp', LAM[:], lam_d[l], W=[r_['lam']])
            b.dma('sp', LDT[:], ldt_d[l], W=[r_['lam']])
            b.dma('sp', BB1[:], B1_d[l], W=[r_['bb']])
            b.dma('sp', BB2[:], B2_d[l], W=[r_['bb']])
            b.dma('sp', CMf[:], C1_d[l], W=[r_['cm']])
            b.dma('sp', CC2[:], C2_d[l], W=[r_['cm']])
            b.dma('sp', S0[:], st0_d[l].rearrange("r p g s -> p r g s"), W=[r_['s0']])

            def T_(i):
                return T[i][:]

            def tt_(out, a0, a1, op, R=(), W=()):
                b.tt('dve', out, a0, a1, op, R=[rt] + list(R), W=[rt] + list(W))

            lre, lim = LAM[:, 0, :], LAM[:, 1, :]
            b.act(T_(0), LDT[:], AF.Exp, R=[r_['lam']], W=[rt])
            tt_(T_(1), lre, T_(0), ALU.mult, R=[r_['lam']])
            tt_(T_(2), lim, T_(0), ALU.mult, R=[r_['lam']])
            b.act(T_(3), T_(1), AF.Exp, R=[rt], W=[rt])
            b.act(T_(4), T_(2), AF.Sin, scale=1.0 / 16, R=[rt], W=[rt])
            b.act(T_(5), T_(2), AF.Sin, bias=HPIc, scale=1.0 / 16, R=[rt, cR['kc']], W=[rt])
            for _ in range(4):
                tt_(T_(6), T_(5), T_(5), ALU.mult)
                tt_(T_(7), T_(4), T_(4), ALU.mult)
                tt_(T_(8), T_(5), T_(4), ALU.mult)
                tt_(T_(5), T_(6), T_(7), ALU.subtract)
                tt_(T_(4), T_(8), T_(8), ALU.add)
            rp = r_['pow']
            b.op('dve', lambda g: g.memset(POWr[:, 0, :], 1.0), W=[rp])
            b.op('dve', lambda g: g.memset(POWi[:, 0, :], 0.0), W=[rp])
            b.tt('dve', POWr[:, 1, :], T_(5), T_(3), ALU.mult, R=[rt], W=[rp])
            b.tt('dve', POWi[:, 1, :], T_(4), T_(3), ALU.mult, R=[rt], W=[rp])
            with ExitStack() as sg:
                PT = [sb(sg, "PT%d" % i, [128, 16, 64], F32) for i in range(2)]
                ptr = Res()
                k = 1
                while k < 32:
                    n = min(k, 32 - k)
                    ar, ai = POWr[:, 1:1 + n, :], POWi[:, 1:1 + n, :]
                    br_ = b.view(POWr, k * 64, [[0, n], [1, 64]])
                    bi_ = b.view(POWi, k * 64, [[0, n], [1, 64]])
                    o_r, o_i = POWr[:, k + 1:k + 1 + n, :], POWi[:, k + 1:k + 1 + n, :]
                    b.tt('dve', PT[0][:, 0:n, :], ar, br_, ALU.mult, R=[rp], W=[ptr])
                    b.tt('dve', PT[1][:, 0:n, :], ai, bi_, ALU.mult, R=[rp], W=[ptr])
                    b.tt('dve', o_r, PT[0][:, 0:n, :], PT[1][:, 0:n, :], ALU.subtract, R=[ptr], W=[rp])
                    b.tt('dve', PT[0][:, 0:n, :], ar, bi_, ALU.mult, R=[rp], W=[ptr])
                    b.tt('dve', PT[1][:, 0:n, :], ai, br_, ALU.mult, R=[rp], W=[ptr])
                    b.tt('dve', o_i, PT[0][:, 0:n, :], PT[1][:, 0:n, :], ALU.add, R=[ptr], W=[rp])
                    k += n
                b.barrier()
            stop(1.1)
            b.cp('dve', L2K[:, 0, :], POWr[0:64, 32, :], R=[rp], W=[r_['l2k']])
            b.cp('dve', L2K[:, 1, :], POWi[0:64, 32, :], R=[rp], W=[r_['l2k']])
            t6, t7, t8 = T[6][0:64, :], T[7][0:64, :], T[8][0:64, :]
            for _ in range(6):
                b.tt('dve', t6, L2K[:, 0, :], L2K[:, 0, :], ALU.mult, R=[r_['l2k']], W=[rt])
                b.tt('dve', t7, L2K[:, 1, :], L2K[:, 1, :], ALU.mult, R=[r_['l2k']], W=[rt])
                b.tt('dve', t8, L2K[:, 0, :], L2K[:, 1, :], ALU.mult, R=[r_['l2k']], W=[rt])
                b.tt('dve', L2K[:, 0, :], t6, t7, ALU.subtract, R=[rt], W=[r_['l2k']])
                b.tt('dve', L2K[:, 1, :], t8, t8, ALU.add, R=[rt], W=[r_['l2k']])
            stop(1.2)
            tt_(T_(6), lre, lre, ALU.mult, R=[r_['lam']])
            tt_(T_(7), lim, lim, ALU.mult, R=[r_['lam']])
            tt_(T_(6), T_(6), T_(7), ALU.add)
            b.op('dve', lambda g: g.reciprocal(out=T_(6), in_=T_(6)), R=[rt], W=[rt])
            b.ts('dve', T_(7), POWr[:, 1, :], -1.0, None, ALU.add, R=[rp], W=[rt])
            tt_(T_(8), T_(7), lre, ALU.mult, R=[r_['lam']])
            tt_(T_(9), POWi[:, 1, :], lim, ALU.mult, R=[r_['lam'], rp])
            tt_(T_(8), T_(8), T_(9), ALU.add)
            tt_(T_(8), T_(8), T_(6), ALU.mult)
            tt_(T_(9), POWi[:, 1, :], lre, ALU.mult, R=[r_['lam'], rp])
            tt_(T_(10), T_(7), lim, ALU.mult, R=[r_['lam']])
            tt_(T_(9), T_(9), T_(10), ALU.subtract)
            tt_(T_(9), T_(9), T_(6), ALU.mult)
            b.ts('dve', T_(9), T_(9), SGN, None, ALU.mult, R=[rt, cR['cst']], W=[rt])
            with ExitStack() as sg:
                U1 = sb(sg, "U1", [128, 64, 16], F32)
                U2 = sb(sg, "U2", [128, 64, 16], F32)
                U3 = sb(sg, "U3", [128, 64, 16], F32)
                ur = Res()
                krb = b.view(T[8], 0, [[1, 64], [0, 16]])
                kib = b.view(T[9], 0, [[1, 64], [0, 16]])
                B1v = b.view(BB1, 0, [[16, 64], [1, 16]])
                B2v = b.view(BB2, 0, [[16, 64], [1, 16]])
                b.tt('dve', U1[:], B1v, krb, ALU.mult, R=[r_['bb'], rt], W=[ur])
                b.tt('dve', U2[:], B2v, kib, ALU.mult, R=[r_['bb'], rt], W=[ur])
                b.tt('dve', U3[:], U1[:], U2[:], ALU.subtract, R=[ur], W=[ur])
                b.tt('dve', U1[:], B2v, krb, ALU.mult, R=[r_['bb'], rt], W=[ur])
                b.tt('dve', U2[:], B1v, kib, ALU.mult, R=[r_['bb'], rt], W=[ur])
                b.tt('dve', U1[:], U1[:], U2[:], ALU.add, R=[ur], W=[ur])
                b.cp('dve', B1v, U3[:], R=[ur], W=[r_['bb']])
                b.ts('dve', B2v, U1[:], SGN, None, ALU.mult, R=[ur, cR['cst']], W=[r_['bb']])
                b.barrier()
            b.ts('dve', CMf[:], CMf[:], SGN, None, ALU.mult, R=[cR['cst']], W=[r_['cm']])
            for c in range(8):
                b.cp('dve', IDCM[:, c, 128:256], CMf[:, c * 128:(c + 1) * 128], R=[r_['cm']], W=[cR['idcm']])

            stop(1.3)

            def cplx_tab(out, t1, t2, tr, m0, nm, ct, X1, X2, xr, powi_sgn):
                pr_ = b.view(POWr, m0 * 64 + ct * 8, [[64, nm], [1, 8], [0, 16]])
                pi_ = b.view(POWi, m0 * 64 + ct * 8, [[64, nm], [1, 8], [0, 16]])
                x1 = b.view(X1, ct * 128, [[0, nm], [16, 8], [1, 16]])
                x2 = b.view(X2, ct * 128, [[0, nm], [16, 8], [1, 16]])
                o4 = lambda t: b.view(t, 0, [[128, nm], [16, 8], [1, 16]])
                b.tt('dve', o4(t1), pr_, x1, ALU.mult, R=[rp, xr], W=[tr])
                b.tt('pool', o4(t2), pi_, x2, ALU.mult, R=[rp, xr], W=[tr])
                b.tt('dve', o4(out), o4(t1), o4(t2), ALU.subtract, R=[tr], W=[tr])

            with ExitStack() as sB:
                t1 = sb(sB, "t1", [128, 8, 128], F32)
                t2 = sb(sB, "t2", [128, 8, 128], F32)
                LB = sb(sB, "LB", [128, 32, 128], BF16)
                LBT = sb(sB, "LBT", [128, 32, 128], BF16)
                KT = sb(sB, "KT", [128, 32, 128], BF16)
                UM = [sb(sB, "UM%d" % i, [128, 32 * NCH], BF16) for i in range(2)]
                lbr, lbtr, ktr, t12r = Res(), Res(), Res(), Res()
                umr = [Res(), Res()]
                def tabB(ct_):
                    for hf in range(4):
                        cplx_tab(LB[:, hf * 8:(hf + 1) * 8, :], t1, t2, t12r, hf * 8, 8, ct_, BB1, BB2, r_['bb'], True)

                tabB(0)
                for ct in range(8):
                    stop(1.4)
                    lbr.w = t12r.w
                    for g4 in range(8):
                        a = acc_next()
                        for i4 in range(4):
                            lag = g4 * 4 + i4
                            o = b.view(PS, a * 1024 + i4 * 256, [[1, 256]])
                            b.mm(o, LB[:, lag, :], IDCM[:, ct, :], start=True, stop=True, R=[t12r, cR['idcm']], W=[ACCr[a]], mark=(i4 == 3))
                        stop(1.41)
                        src_t = b.view(PS, a * 1024, [[256, 4], [1, 128]])
                        src_k = b.view(PS, a * 1024 + 128, [[256, 4], [1, 128]])
                        b.cp('dve', LBT[:, g4 * 4:g4 * 4 + 4, :], src_t, R=[ACCr[a]], W=[lbtr])
                        stop(1.42)
                        b.tt('dve', KT[:, g4 * 4:g4 * 4 + 4, :], src_k, BD4[:], ALU.mult,
                             R=[ACCr[a], cR['idcm']], W=[ktr])
                    stop(1.45)
                    b.stt('dve', KT[:, 0, :], IDCM[:, ct, 0:128], DV(ct), KT[:, 0, :], ALU.mult, ALU.add, R=[cR['idcm'], cR['vec']], W=[ktr])
                    stop(1.5)
                    for gg in range(8):
                        g = ct * 8 + gg
                        u = UM[gg % 2]
                        b.ts('dve', u[:], YU[ct][:], MCOL(gg), None, ALU.mult, R=[YUr[ct], cR['cst']], W=[umr[gg % 2]])
                        if gg % 4 == 0:
                            a = acc_next()
                        o = b.view(PS, a * 1024 + (gg % 4) * NCH, [[1, NCH]])
                        for r in range(32):
                            b.mm(o, LBT[:, 31 - r, :], u[:, r * NCH:(r + 1) * NCH], start=(r == 0), stop=(r == 31),
                                 R=[lbtr, umr[gg % 2]], W=[ACCr[a]], mark=(r == 31))
                        if gg % 4 == 3:
                            src = b.view(PS, a * 1024, [[NCH, 4], [1, NCH]])
                            b.cp('act', SSB[:, g - 3:g + 1, :], src, R=[ACCr[a]], W=[r_['ssb']])
                    if ct + 1 < 8:
                        tabB(ct + 1)
                    stop(1.6)
                    allacc = ACCr
                    for lag in range(32):
                        for bkk in range(5):
                            r0 = max(lag, 7 * bkk)
                            r1 = min(7 * bkk + 7, 32)
                            if r0 >= r1:
                                continue
                            lastlag = r1 - 1
                            o = b.view(PS, bkk * 512 + (r0 - 7 * bkk) * NCH, [[1, (r1 - r0) * NCH]])
                            rhs = YU[ct][:, (r0 - lag) * NCH:(r1 - lag) * NCH]
                            b.mm(o, KT[:, lag, :], rhs, start=(lag == 0), stop=(lag == lastlag),
                                 R=[ktr, YUr[ct]], W=allacc, mark=(lag == 31))
                    for bkk in range(5):
                        nr = min(7, 32 - 7 * bkk)
                        for bq in range(NB):
                            src = b.view(PS, bkk * 512 + 16 * bq, [[NCH, nr], [1, 16]])
                            dst = b.view(YU[ct], BW * bq + 7 * bkk, [[1, nr], [32, 16]])
                            b.cp('act' if bkk % 2 == 0 else 'dve', dst, src, R=allacc, W=[YUr[ct]])
                        src = b.view(PS, bkk * 512 + 64, [[NCH, nr], [1, NB]])
                        dst = b.view(YU[ct], 512 + 7 * bkk, [[1, nr], [BW, NB]])
                        b.cp('act' if bkk % 2 == 0 else 'dve', dst, src, R=allacc, W=[YUr[ct]])
                    stop(1.7)
                b.barrier()
            stop(2)

            b.dma('sp', IMT[:], SSB[64:128, :, :], R=[r_['ssb']], W=[r_['imt']])
            LR, LI = POWr[0:64, 32, :], POWi[0:64, 32, :]
            ta, tb_, tc_, td = T[0][0:64, :], T[1][0:64, :], T[2][0:64, :], T[3][0:64, :]

            def step(re_o, im_o, re_i, im_i, sre, sim, lr=LR, li=LI, R=()):
                R = [rt, rp, r_['st'], r_['ssb'], r_['imt']] + list(R)
                W = [rt, r_['st']]
                b.tt('dve', ta, re_i, lr, ALU.mult, R=R, W=W)
                b.tt('dve', tb_, im_i, li, ALU.mult, R=R, W=W)
                b.tt('dve', tc_, im_i, lr, ALU.mult, R=R, W=W)
                b.tt('dve', td, re_i, li, ALU.mult, R=R, W=W)
                b.tt('dve', ta, ta, tb_, ALU.subtract, R=R, W=W)
                b.tt('dve', tc_, tc_, td, ALU.add, R=R, W=W)
                b.tt('dve', re_o, ta, sre, ALU.add, R=R, W=W)
                b.tt('dve', im_o, tc_, sim, ALU.add, R=R, W=W)

            SRE = lambda c: SSB[0:64, :, c]
            SIM = lambda c: IMT[:, :, c]
            b.cp('dve', ST[0][:], SRE(0), R=[r_['ssb']], W=[r_['st']])
            b.cp('dve', ST[1][:], SIM(0), R=[r_['imt']], W=[r_['st']])
            cur = 0
            for c in range(1, 64):
                nxt = 1 - cur
                step(ST[2 * nxt][:], ST[2 * nxt + 1][:], ST[2 * cur][:], ST[2 * cur + 1][:], SRE(c), SIM(c))
                cur = nxt
            b.dma('sp', EXI[0:64, 0:64], ST[2 * cur][:], R=[r_['st']], W=[EXIr])
            b.dma('sp', EXI[64:128, 0:64], ST[2 * cur + 1][:], R=[r_['st']], W=[EXIr])
            b.dma('sp', EXI[:, 64:80], ZL[:], R=[cR['zl']], W=[EXIr])
            if not b.stopped:
                b._deps('pool', [EXIr], [EXOr])
                b.ccn += 1
            if not b.dry and not b.stopped:
                nc.gpsimd.collective_compute("AllGather", ALU.bypass, replica_groups=[list(range(NCORE))],
                                             ins=[EXI.ap().opt()], outs=[EXO.ap().opt()]).then_inc(b.sem['cc'], 1)
            if not b.stopped:
                b._upd(('cc', b.ccn), [EXIr], [EXOr])
            exv = EXO.ap().rearrange("(k q) c -> q k c", q=128)
            b.dma('sp', SF[:], exv, R=[EXOr], W=[r_['sf']])
            b.dma('sp', SFI[:], exv[64:128, :, 0:64], R=[EXOr], W=[r_['sfi']])
            AR, AI = ST[2 * cur][:], ST[2 * cur + 1][:]
            NR, NI = ST[2 * (1 - cur)][:], ST[2 * (1 - cur) + 1][:]
            b.op('dve', lambda g: g.memset(AR, 0.0), R=[r_['st']], W=[r_['st']])
            b.op('dve', lambda g: g.memset(AI, 0.0), R=[r_['st']], W=[r_['st']])
            for k in range(NCORE - 1):
                step(NR, NI, AR, AI, SF[0:64, k, 0:64], SFI[:, k, :], lr=L2K[:, 0, :], li=L2K[:, 1, :],
                     R=[r_['sf'], r_['sfi'], r_['l2k']])
                Rr = [rt, r_['st'], cR['cmk']]
                b.tt('dve', NR, NR, AR, ALU.subtract, R=Rr, W=[r_['st']])
                b.tt('dve', NI, NI, AI, ALU.subtract, R=Rr, W=[r_['st']])
                b.stt('dve', AR, NR, CMK[0:64, k:k + 1], AR, ALU.mult, ALU.add, R=Rr, W=[r_['st']])
                b.stt('dve', AI, NI, CMK[0:64, k:k + 1], AI, ALU.mult, ALU.add, R=Rr, W=[r_['st']])
            b.op('dve', lambda g: g.memset(ZH[:], 0.0), W=[cR['zh']])
            zhf = b.view(ZH, 0, [[1, 16]])
            for k in range(NCORE):
                b.stt('dve', zhf, SF[:, k, 64:80], CMK[:, 8 + k:9 + k], zhf, ALU.mult, ALU.add,
                      R=[r_['sf'], cR['cmk']], W=[cR['zh']])
            rsp = [r_['sp'], r_['spi']]
            b.cp('dve', SP[0:64, :, 0], AR, R=[r_['st']], W=rsp)
            b.cp('dve', SPI[:, :, 0], AI, R=[r_['st']], W=rsp)
            for c in range(64):
                step(NR, NI, AR, AI, SRE(c), SIM(c))
                AR, NR = NR, AR
                AI, NI = NI, AI
                if c < 63:
                    b.cp('dve', SP[0:64, :, c + 1], AR, R=[r_['st']], W=rsp)
                    b.cp('dve', SPI[:, :, c + 1], AI, R=[r_['st']], W=rsp)
            b.dma('sp', ssmp_o[l, 0], AR, R=[r_['st']])
            b.dma('sp', ssmp_o[l, 1], AI, R=[r_['st']])
            b.dma('sp', convp_o[l], ZL[:], R=[cR['zl']])
            b.cp('dve', SP[0:64, :, 64:68], S0[:, 0, :, :], R=[r_['s0']], W=rsp)
            b.cp('dve', SPI[:, :, 64:68], S0[:, 1, :, :], R=[r_['s0']], W=rsp)
            with ExitStack() as sg:
                Q = [sb(sg, "Q%d" % i, [64, 64, NB], F32) for i in range(4)]
                qr = Res()
                lrb = b.view(POWr, 32 * 64, [[1, 64], [0, NB]], npart=64)
                lib = b.view(POWi, 32 * 64, [[1, 64], [0, NB]], npart=64)
                Rq = [qr, r_['s0'], rp, r_['ssb'], r_['imt']]
                b.tt('dve', Q[0][:], S0[:, 0, :, :], lrb, ALU.mult, R=Rq, W=[qr])
                b.tt('dve', Q[1][:], S0[:, 1, :, :], lib, ALU.mult, R=Rq, W=[qr])
                b.tt('dve', Q[2][:], S0[:, 1, :, :], lrb, ALU.mult, R=Rq, W=[qr])
                b.tt('dve', Q[3][:], S0[:, 0, :, :], lib, ALU.mult, R=Rq, W=[qr])
                b.tt('dve', Q[0][:], Q[0][:], Q[1][:], ALU.subtract, R=Rq, W=[qr])
                b.tt('dve', Q[2][:], Q[2][:], Q[3][:], ALU.add, R=Rq, W=[qr])
                b.tt('dve', SN[:, 0, :, :], Q[0][:], SSB[0:64, :, 64:68], ALU.add, R=Rq, W=[r_['sn']])
                b.tt('dve', SN[:, 1, :, :], Q[2][:], IMT[:, :, 64:68], ALU.add, R=Rq, W=[r_['sn']])
                b.dma('sp', ssms_o[l].rearrange("r p g s -> p r g s"), SN[:], R=[r_['sn']])
                b.barrier()
            b.dma('sp', SP[64:128, :, :], SPI[:], R=[r_['spi']], W=[r_['sp']])
            b.barrier()
            stop(3)

            with ExitStack() as sD:
                t1 = sb(sD, "t1d", [128, 8, 128], F32)
                t2 = sb(sD, "t2d", [128, 8, 128], F32)
                CL = sb(sD, "CL", [128, 32, 128], BF16)
                ZT = [sb(sD, "ZT%d" % i, [128, 8, 128], BF16) for i in range(2)]
                YT = sb(sD, "YT", [128, TC], F32)
                ztr = [Res(), Res()]
                t12r, ytr = Res(), Res()
                for i in range(2):
                    b.op('pool', lambda g, i=i: g.memset(ZT[i][:], 0.0), W=[ztr[i]])
                allacc = ACCr
                def tabD(ct_):
                    for hf in range(4):
                        cplx_tab(CL[:, hf * 8:(hf + 1) * 8, :], t1, t2, t12r, 1 + hf * 8, 8, ct_, CMf, CC2, r_['cm'], False)

                for ct in range(8):
                    tabD(ct)
                    for r in range(32):
                        z = ZT[r % 2]
                        zd = b.view(z, 0, [[144, 8], [1, 16]])
                        b.cp('pool', zd, b.view(CL, r * 128, [[16, 8], [1, 16]]), R=[t12r], W=[ztr[r % 2]])
                        bkk, ro = r // 7, r % 7
                        o = b.view(PS, bkk * 512 + ro * NCH, [[1, NCH]])
                        for gg in range(8):
                            b.mm(o, z[:, gg, :], SP[:, ct * 8 + gg, :], start=(gg == 0), stop=(gg == 7),
                                 R=[ztr[r % 2], r_['sp']], W=allacc, mark=(gg == 7))
                    for bkk in range(5):
                        nr = min(7, 32 - 7 * bkk)
                        for bq in range(NB):
                            src = b.view(PS, bkk * 512 + 16 * bq, [[NCH, nr], [1, 16]])
                            yi = b.view(YU[ct], BW * bq + 7 * bkk, [[1, nr], [32, 16]])
                            yo = b.view(YT, BW * bq + 7 * bkk, [[1, nr], [32, 16]])
                            b.tt('dve', yo, src, yi, ALU.add, R=allacc + [YUr[ct]], W=[ytr])
                        src = b.view(PS, bkk * 512 + 64, [[NCH, nr], [1, NB]])
                        yi = b.view(YU[ct], 512 + 7 * bkk, [[1, nr], [BW, NB]])
                        yo = b.view(YT, 512 + 7 * bkk, [[1, nr], [BW, NB]])
                        b.tt('dve', yo, src, yi, ALU.add, R=allacc + [YUr[ct]], W=[ytr])
                    b.act(YU[ct][:], YT[:], AF.Gelu_apprx_tanh, R=[ytr], W=[YUr[ct]])
            b.barrier()
            stop(4)

        with ExitStack() as sp_:
            X32 = [sb(sp_, "X32_%d" % i, [128, BW], F32) for i in range(DT)]
            X32r = [Res() for _ in range(DT)]
            XN = [sb(sp_, "XN_%d" % i, [128, BW], BF16) for i in range(DT)]
            XNr = [Res() for _ in range(DT)]
            SQ = [sb(sp_, "SQ%d" % i, [128, BW], BF16) for i in range(2)]
            SQr = [Res(), Res()]
            RS = sb(sp_, "RS", [128, BW], F32)
            RSr = Res()
            GT = [sb(sp_, "GT%d" % i, [128, BW], BF16) for i in range(4)]
            GTr = [Res() for _ in range(4)]
            FT_ = [sb(sp_, "FT%d" % i, [128, BW], F32) for i in range(2)]
            FTr = [Res() for _ in range(2)]
            gti = [0]
            fti = [0]
            P2 = {}

            def pt(name, n, shape, dt):
                if name not in P2:
                    P2[name] = ([sb(sp_, name + str(i), shape, dt) for i in range(n)], [Res() for _ in range(n)])
                return P2[name]

            LNG = sb(sp_, "LNG", [128, 1024], BF16)
            BSB = sb(sp_, "BSB", [128, 8, BW], BF16)
            WTM = sb(sp_, "WTM", [128, 8, 128], BF16)
            for k_ in ('lng', 'bsb', 'wtm'):
                cR[k_] = Res()
            b.dma('pool', LNG[:], lnv_d[l], W=[cR['lng']])
            b.dma('pool', WTM[:], wsgT_d[l], W=[cR['wtm']])
            b.tt('dve', WTM[:], WTM[:], b.view(TRIb, 0, [[0, 8], [1, 128]]), ALU.mult, R=[cR['tri']], W=[cR['wtm']])
            b.dma('pool', BSB[:, :, 0:128], bsgu_d[l], W=[cR['bsb']])
            for tt_ in range(1, 4):
                b.cp('dve', BSB[:, :, tt_ * 128:(tt_ + 1) * 128], BSB[:, :, 0:128], R=[cR['bsb']], W=[cR['bsb']])
            b.cp('dve', BSB[:, :, 512:544], BSB[:, :, 0:32], R=[cR['bsb']], W=[cR['bsb']])

            def gt_next():
                i = gti[0]
                gti[0] = (i + 1) % 4
                return i

            def ft_next():
                i = fti[0]
                fti[0] = (i + 1) % 2
                return i

            def rstd_from_acc(a):
                b.act(hv(RS), accv(a), AF.Sqrt, bias=EPSc, R=[ACCr[a], cR['kc']], W=[RSr])
                b.op('dve', lambda g: g.reciprocal(out=RS[:], in_=RS[:]), R=[RSr], W=[RSr])

            def sumsq_to_acc(src, srcr):
                a = acc_next()
                for ft in range(DT):
                    q = ft % 2
                    b.act(SQ[q][:], src[ft][:], AF.Square, R=[srcr[ft]], W=[SQr[q]])
                    for h in range(2):
                        b.mm(acch(a, h, 0, HW), ONESB[:], SQ[q][:, h * HW:(h + 1) * HW], start=(ft == 0), stop=(ft == DT - 1),
                             R=[SQr[q], cR['ones']], W=[ACCr[a]], mark=(h == 1))
                return a

            def norm_into(dst, dstr, gfun, gres):
                a = sumsq_to_acc(X32, X32r)
                rstd_from_acc(a)
                for ft in range(DT):
                    b.stt('dve', dst[ft][:], X32[ft][:], gfun(ft), RS[:], ALU.mult, ALU.mult,
                          R=[X32r[ft], RSr, gres], W=[dstr[ft]])

            for bk in range(NB):
                for ft in range(DT):
                    rr = [XSr[bk][ft]] if l > 0 else []
                    b.dma('sp', X32[ft][:], xsrc[ft, :, bk * BW:(bk + 1) * BW], R=rr, W=[X32r[ft]])
                norm_into(XN, XNr, G1, cR['vec'])

                if True:
                    MG, MGr = pt("MG", DT, [128, BW], BF16)
                    BI, BIr = pt("BI", 8, [128, BW], BF16)

                    def out_branch(e, wname, A_, Ar_):
                        for pr in range(8):
                            wg, wgr = need('gate', l, 0, 16, e * 2048 + pr * 256)
                            gis = []
                            for t2 in range(2):
                                n = pr * 2 + t2
                                ag = acc_next()
                                proj(ag, wg, wgr, t2 * 128, XN, XNr, 16)
                                gi = gt_next()
                                gis.append(gi)
                                b.act(hv(GT[gi]), accv(ag), AF.Sigmoid, bias=BGATE(e, n), R=[ACCr[ag], cR['vec']], W=[GTr[gi]])
                            wt, wr = need(wname, l, 0, 8, pr * 256)
                            for t2 in range(2):
                                n = pr * 2 + t2
                                gi = gis[t2]
                                ay = acc_next()
                                proj(ay, wt, wr, t2 * 128, A_, Ar_, 8)
                                if e == 2:
                                    b.tt('dve', hv(MG[n]), accv(ay), hv(GT[gi]), ALU.mult, R=[ACCr[ay], GTr[gi]], W=[MGr[n]])
                                else:
                                    fi = ft_next()
                                    b.tt('dve', hv(FT_[fi]), accv(ay), hv(GT[gi]), ALU.mult, R=[ACCr[ay], GTr[gi]], W=[FTr[fi]])
                                    b.tt('pool', MG[n][:], MG[n][:], FT_[fi][:], ALU.add, R=[FTr[fi]], W=[MGr[n]])

                    if True:
                        VGB, vr4 = pt("VGB", 4, [128, 1024], BF16)
                        (VGS,), (vrs,) = pt("VGS", 1, [32, 1024], F32)
                        (VNS,), _u = pt("VNS", 1, [32, 1024], BF16)
                        VGr = vr4 + [vrs]
                        UG, UGr = pt("UG", 8, [128, BW], BF16)
                        AC, ACr = BI, BIr
                        (VSQ,), (vsqr,) = pt("VSQ", 1, [128, 1024], F32)
                        (STT,), (sttr,) = pt("STT", 1, [128, 8], F32)
                        VGt = lambda i: (VGB[i][:, :] if i < 4 else VGS[:, :])
                        for pc in range(4):
                            wt, wr = need('in', l, 0, 16, 5120 + pc * 256)
                            for tt_i in range(5):
                                npt = 128 if tt_i < 4 else 32
                                c0 = tt_i * 128
                                a = acc_next()
                                o = b.view(PS, a * 1024, [[1, 256]], npart=npt)
                                for kt in range(DT):
                                    b.mm(o, XN[kt][:, c0:c0 + npt], wt[:, kt, :], start=(kt == 0), stop=(kt == DT - 1),
                                         R=[wr, XNr[kt]], W=[ACCr[a]], mark=(kt == DT - 1))
                                b.act(VGt(tt_i)[:, pc * 256:(pc + 1) * 256], o, AF.Gelu_apprx_tanh, R=[ACCr[a]], W=[VGr[tt_i]])
                        for tt_i in range(5):
                            npt = 128 if tt_i < 4 else 32
                            vg = VGt(tt_i)
                            vq = VSQ[0:npt, :]
                            s_ = lambda j: STT[0:npt, j:j + 1]
                            Rv = [VGr[tt_i], vsqr, sttr]
                            b.op('dve', lambda g: g.reduce_sum(out=s_(0), in_=vg, axis=AX.X), R=Rv, W=[sttr])
                            b.tt('pool', vq, vg, vg, ALU.mult, R=Rv, W=[vsqr])
                            b.op('dve', lambda g: g.reduce_sum(out=s_(1), in_=vq, axis=AX.X), R=Rv, W=[sttr])
                            b.ts('dve', s_(2), s_(0), 1.0 / 1024, None, ALU.mult, R=Rv, W=[sttr])
                            b.tt('dve', s_(3), s_(2), s_(2), ALU.mult, R=Rv, W=[sttr])
                            b.stt('dve', s_(4), s_(1), 1.0 / 1024, s_(3), ALU.mult, ALU.subtract, R=Rv, W=[sttr])
                            b.act(s_(5), s_(4), AF.Sqrt, bias=KC[0:npt, 0:1], R=Rv + [cR['kc']], W=[sttr])
                            b.op('dve', lambda g: g.reciprocal(out=s_(6), in_=s_(5)), R=Rv, W=[sttr])
                            b.ts('dve', vq, vg, s_(2), s_(6), ALU.subtract, ALU.mult, R=Rv, W=[vsqr])
                            b.tt('dve', vg, vq, LNG[0:npt, :], ALU.mult, R=[vsqr, cR['lng']], W=[VGr[tt_i]])
                            if tt_i == 4:
                                b.cp('act', VNS[:, :], vg, R=[VGr[tt_i]], W=[VGr[tt_i]])
                                b.dma('sp', sguv_o[l, bk], vg, R=[VGr[tt_i]])
                        for pr in range(4):
                            wt, wr = need('in', l, 0, 16, 4096 + pr * 256)
                            for t2 in range(2):
                                ct = pr * 2 + t2
                                a = acc_next()
                                proj(a, wt, wr, t2 * 128, XN, XNr, 16)
                                b.act(hv(UG[ct]), accv(a), AF.Gelu_apprx_tanh, R=[ACCr[a]], W=[UGr[ct]])
                        for h in range(8):
                            a = acc_next()
                            for tt_i in range(4):
                                b.mm(acch(a, tt_i // 2, (tt_i % 2) * 128, (tt_i % 2) * 128 + 128), VGB[tt_i][:, h * 128:(h + 1) * 128],
                                     WTM[:, h, :], start=True, stop=True, R=[VGr[tt_i], cR['wtm']], W=[ACCr[a]], mark=False)
                            b.mm(acch(a, 1, 256, 288), VNS[0:32, h * 128:(h + 1) * 128], WTM[0:32, h, 0:32],
                                 start=True, stop=True, R=[VGr[4], cR['wtm']], W=[ACCr[a]], mark=True)
                            fi = ft_next()
                            b.tt('dve', FT_[fi][:, 0:256], acch(a, 0, 0, 256), BSB[:, h, 0:256], ALU.add, R=[ACCr[a], cR['bsb']], W=[FTr[fi]])
                            b.tt('dve', FT_[fi][:, 256:544], acch(a, 1, 0, 288), BSB[:, h, 256:544], ALU.add, R=[ACCr[a], cR['bsb']], W=[FTr[fi]])
                            b.tt('pool', AC[h][:], FT_[fi][:], UG[h][:], ALU.mult, R=[FTr[fi], UGr[h]], W=[ACr[h]])
                        out_branch(2, 'sgo', AC, ACr)
                        stop(5)

                    if True:
                        AA, AAr = BI, BIr
                        ZP, ZPr = pt("ZP", 2, [128, 548], F32)
                        CV, CVr = pt("CV", 2, [128, 548], F32)
                        (CS,), (csr,) = pt("CS", 1, [128, 8, 2], F32)
                        (SH,), (shr,) = pt("SH", 1, [128, 8, 2], F32)
                        b.dma('sp', SH[:], convc_d[l, bk], W=[shr])
                        for pr in range(4):
                            wc, wcr = need('in', l, 0, 16, 1024 + pr * 256)
                            gis = []
                            for t2 in range(2):
                                a1 = acc_next()
                                proj(a1, wc, wcr, t2 * 128, XN, XNr, 16)
                                gi = gt_next()
                                gis.append(gi)
                                b.cp('act', hv(GT[gi]), accv(a1), R=[ACCr[a1]], W=[GTr[gi]])
                            whh, whr = need('in', l, 0, 16, 2048 + pr * 256)
                            for t2 in range(2):
                                ct = pr * 2 + t2
                                zi = ct % 2
                                zp, cv = ZP[zi], CV[zi]
                                gi = gis[t2]
                                a2 = acc_next()
                                proj(a2, whh, whr, t2 * 128, XN, XNr, 16)
                                for (hh, lo, hi, c0, c1) in pieces(0, 512):
                                    b.tt('dve', zp[:, 2 + c0:2 + c1], acch(a2, hh, lo, hi), GT[gi][:, c0:c1], ALU.mult,
                                         R=[ACCr[a2], GTr[gi]], W=[ZPr[zi]])
                                b.tt('dve', zp[:, 516:548], acch(a2, 1, 512 - HW, BW - HW), GT[gi][:, 512:544], ALU.mult,
                                     R=[ACCr[a2], GTr[gi]], W=[ZPr[zi]])
                                b.cp('pool', zp[:, 0:2], ZH[:, ct, :], R=[cR['zh']], W=[ZPr[zi]])
                                b.cp('pool', zp[:, 514:516], SH[:, ct, :], R=[shr], W=[ZPr[zi]])
                                b.cp('pool', ZH[:, ct, :], zp[:, 512:514], R=[ZPr[zi]], W=[cR['zh']])
                                b.cp('pool', CS[:, ct, :], zp[:, 546:548], R=[ZPr[zi]], W=[csr])
                                b.ts('dve', cv[:, 0:546], zp[:, 2:548], CW(2, ct), None, ALU.mult, R=[ZPr[zi], cR['vec']], W=[CVr[zi]])
                                for kk in (1, 0):
                                    b.stt('dve', cv[:, 0:546], zp[:, kk:kk + 546], CW(kk, ct), cv[:, 0:546], ALU.mult, ALU.add,
                                          R=[ZPr[zi], cR['vec']], W=[CVr[zi]])
                            wb_, wbr = need('in', l, 0, 16, pr * 256)
                            for t2 in range(2):
                                ct = pr * 2 + t2
                                zi = ct % 2
                                cv = CV[zi]
                                a3 = acc_next()
                                proj(a3, wb_, wbr, t2 * 128, XN, XNr, 16)
                                for (hh, lo, hi, c0, c1) in pieces(0, 512):
                                    b.tt('dve', AA[ct][:, c0:c1], acch(a3, hh, lo, hi), cv[:, c0:c1], ALU.mult,
                                         R=[ACCr[a3], CVr[zi]], W=[AAr[ct]])
                                b.tt('dve', AA[ct][:, 512:544], acch(a3, 1, 512 - HW, BW - HW), cv[:, 514:546], ALU.mult,
                                     R=[ACCr[a3], CVr[zi]], W=[AAr[ct]])
                        b.dma('sp', convs_o[l, bk], CS[:], R=[csr])
                        out_branch(0, 'cvo', AA, AAr)
                        stop(6)

                    if True:
                        AB, ABr = BI, BIr
                        YS = [YU[ct][:, bk * BW:(bk + 1) * BW] for ct in range(8)]
                        for pr in range(4):
                            wt, wr = need('glu', l, 0, 8, pr * 256)
                            for t2 in range(2):
                                ct = pr * 2 + t2
                                a = acc_next()
                                proj(a, wt, wr, t2 * 128, YS, YUr, 8)
                                gi = gt_next()
                                b.act(hv(GT[gi]), accv(a), AF.Sigmoid, bias=BGLU(ct), R=[ACCr[a], cR['vec']], W=[GTr[gi]])
                                b.tt('pool', AB[ct][:], GT[gi][:], YS[ct], ALU.mult, R=[GTr[gi], YUr[ct]], W=[ABr[ct]])
                        out_branch(1, 'sso', AB, ABr)
                        stop(7)

                    for pr in range(8):
                        wt, wr = need('o', l, 0, 16, pr * 256)
                        for t2 in range(2):
                            n = pr * 2 + t2
                            a = acc_next()
                            proj(a, wt, wr, t2 * 128, MG, MGr, 16)
                            b.tt('dve', hv(X32[n]), accv(a), hv(X32[n]), ALU.add, R=[ACCr[a]], W=[X32r[n]])

                norm_into(XN, XNr, G2, cR['vec'])
                if True:
                    NF = 22
                    A_ = MG + BI[0:NF - DT]
                    Ar_ = MGr + BIr[0:NF - DT]
                    for hf in range(2):
                        for pr in range(NF // 2):
                            f0 = hf * NF * 128 + pr * 256
                            wg, wgr = need('fi', l, 0, 16, f0)
                            gis = []
                            for t2 in range(2):
                                ag = acc_next()
                                proj(ag, wg, wgr, t2 * 128, XN, XNr, 16)
                                gi = gt_next()
                                gis.append(gi)
                                b.act(hv(GT[gi]), accv(ag), AF.Silu, R=[ACCr[ag]], W=[GTr[gi]])
                            wu, wur = need('fi', l, 0, 16, DFF + f0)
                            for t2 in range(2):
                                f = pr * 2 + t2
                                gi = gis[t2]
                                au = acc_next()
                                proj(au, wu, wur, t2 * 128, XN, XNr, 16)
                                b.tt('dve', hv(A_[f]), accv(au), hv(GT[gi]), ALU.mult, R=[ACCr[au], GTr[gi]], W=[Ar_[f]])
                        for pr in range(8):
                            w1, w1r = need('fo', l, hf * NF, 16, pr * 256)
                            aa = [acc_next(), acc_next()]
                            for t2 in range(2):
                                proj(aa[t2], w1, w1r, t2 * 128, A_[0:16], Ar_[0:16], 16, first=True, last=False)
                            w2, w2r = need('fo', l, hf * NF + 16, NF - 16, pr * 256)
                            for t2 in range(2):
                                n = pr * 2 + t2
                                proj(aa[t2], w2, w2r, t2 * 128, A_[16:NF], Ar_[16:NF], NF - 16, first=False, last=True)
                                b.tt('dve', hv(X32[n]), accv(aa[t2]), hv(X32[n]), ALU.add, R=[ACCr[aa[t2]]], W=[X32r[n]])

                if not last_layer:
                    for ft in range(DT):
                        b.dma('sp', XS[ft, :, bk * BW:(bk + 1) * BW], X32[ft][:], R=[X32r[ft]], W=[XSr[bk][ft]])
                else:
                    a = sumsq_to_acc(X32, X32r)
                    rstd_from_acc(a)
                    for ft in range(DT):
                        b.stt('dve', X32[ft][:], X32[ft][:], GF[:, ft:ft + 1], RS[:], ALU.mult, ALU.mult,
                              R=[RSr, cR['gf']], W=[X32r[ft]])
                        b.dma('sp', yT[ft, :, bk * BW:(bk + 1) * BW], X32[ft][:], R=[X32r[ft]])
            b.barrier()

    body()
    b.stopped = False
    b.barrier()
    for k, v in b.semval.items():
        b._wait('sp', k, v)
    es.close()
    return b


def build_nc():
    plan = []
    nc0 = bass.Bass("TRN2", target_bir_lowering=False)
    _build(nc0, True, plan)
    nc = bass.Bass("TRN2", target_bir_lowering=False)
    _build(nc, False, plan)
    return nc


_PROBE_ONLY = False


def _consts():
    c = np.zeros((128, 4, 128), np.float32)
    s = np.arange(128)
    c[:, 0, :] = (s[:, None] <= s[None, :]).astype(np.float32)
    c[:, 1, :] = ((s[:, None] // 16) == (s[None, :] // 16)).astype(np.float32)
    c[:, 2, :] = np.eye(128, dtype=np.float32)
    for gg in range(8):
        c[:, 3, gg] = (s // 16 == gg)
    c[:, 3, 8] = np.where(s < 64, 1.0, -1.0)
    return c


def kernel(x_prompt, x_sample, cache_conv, state_ssm_re, state_ssm_im, norm_mix_g, w_in, conv_w,
           w_conv_out, ssm_lam_re, ssm_lam_im, ssm_log_dt, ssm_b_re, ssm_b_im, ssm_c_re, ssm_c_im,
           ssm_d, w_glu, b_glu, w_ssm_out, ln_v_g, w_sgu_s, b_sgu_s, w_sgu_out, w_gate, b_gate, w_o,
           norm_ffn_g, w_ffn_in, w_ffn_out, norm_final_g):
    f = lambda a: np.ascontiguousarray(np.asarray(a, dtype=np.float32))
    x_prompt, x_sample = f(x_prompt), f(x_sample)
    L = DEPTH
    pl = lambda v: f(np.asarray(v, np.float32).reshape(L, -1, 128).transpose(0, 2, 1))
    vecs = np.zeros((L, 128, NV), np.float32)
    vecs[:, :, V_G1:V_G1 + 16] = pl(norm_mix_g)
    vecs[:, :, V_G2:V_G2 + 16] = pl(norm_ffn_g)
    cw = np.asarray(conv_w, np.float32).reshape(L, 3, 8, 128).transpose(0, 3, 1, 2)
    vecs[:, :, V_CW:V_CW + 24] = cw.reshape(L, 128, 24)
    vecs[:, :, V_D:V_D + 8] = pl(ssm_d)
    vecs[:, :, V_BGLU:V_BGLU + 8] = pl(b_glu)
    bg = np.asarray(b_gate, np.float32).reshape(L, 3, 16, 128).transpose(0, 3, 1, 2)
    vecs[:, :, V_BGATE:V_BGATE + 48] = bg.reshape(L, 128, 48)
    gF = f(np.asarray(norm_final_g, np.float32).reshape(16, 128).T)
    lnv = f(np.broadcast_to(np.asarray(ln_v_g, np.float32)[:, None, :], (L, 128, 1024)))
    bsgu = f(np.broadcast_to(np.asarray(b_sgu_s, np.float32)[:, None, :, :], (L, 128, 8, 128)))
    wsgT = f(np.asarray(w_sgu_s, np.float32).transpose(0, 3, 1, 2))
    lre = np.asarray(ssm_lam_re, np.float32).transpose(0, 2, 1)
    lim = np.asarray(ssm_lam_im, np.float32).transpose(0, 2, 1)
    lam = np.stack([lre, lim], axis=2)
    lam = f(np.concatenate([lam, lam], axis=1))
    ldt = f(np.broadcast_to(np.asarray(ssm_log_dt, np.float32)[:, None, :], (L, 128, 64)))
    br = np.asarray(ssm_b_re, np.float32).transpose(0, 2, 1, 3).reshape(L, 64, 1024)
    bi = np.asarray(ssm_b_im, np.float32).transpose(0, 2, 1, 3).reshape(L, 64, 1024)
    cr = np.asarray(ssm_c_re, np.float32).transpose(0, 3, 1, 2).reshape(L, 64, 1024)
    ci = np.asarray(ssm_c_im, np.float32).transpose(0, 3, 1, 2).reshape(L, 64, 1024)
    B1 = f(np.concatenate([br, bi], axis=1))
    B2 = f(np.concatenate([bi, br], axis=1))
    C1 = f(np.concatenate([cr, ci], axis=1))
    C2 = f(np.concatenate([ci, cr], axis=1))
    shared = {
        "vecs": vecs, "gF": gF, "lnv": lnv, "bsgu": bsgu, "wsgT": wsgT, "cst": _consts(),
        "lam": lam, "ldt": ldt, "B1": B1, "B2": B2, "C1": C1, "C2": C2,
    }
    wsrc = {"w_in": w_in, "w_gate": np.asarray(w_gate).reshape(L, D, 6144), "w_conv_out": w_conv_out, "w_glu": w_glu,
            "w_ssm_out": w_ssm_out, "w_sgu_out": w_sgu_out, "w_o": w_o, "w_ffn_in": w_ffn_in, "w_ffn_out": w_ffn_out}
    for nm, arr in wsrc.items():
        arr = np.asarray(arr)
        for l_ in range(RUN_DEPTH):
            shared["%s_%d" % (nm, l_)] = f(arr[l_]) if (KSTOP > 4 or nm == 'w_in') else np.zeros((128, 256), np.float32)
    in_maps = []
    xpT = x_prompt[0].T
    for k in range(NCORE):
        xc = np.empty((D, TC), np.float32)
        for bk in range(NB):
            t0 = k * 2048 + bk * 512
            xc[:, bk * BW:bk * BW + 512] = xpT[:, t0:t0 + 512]
            xc[:, bk * BW + 512:(bk + 1) * BW] = x_sample[4 * k + bk].T
        m = dict(shared)
        m["xT"] = f(xc.reshape(DT, 128, TC))
        cc = np.asarray(cache_conv, np.float32)[:, 4 * k:4 * k + 4]
        m["convc"] = f(cc.reshape(L, NB, 2, 8, 128).transpose(0, 1, 4, 3, 2))
        sr = np.asarray(state_ssm_re, np.float32)[:, 4 * k:4 * k + 4]
        si = np.asarray(state_ssm_im, np.float32)[:, 4 * k:4 * k + 4]
        st = np.stack([sr, si], axis=1)
        m["st0"] = f(st.transpose(0, 1, 4, 3, 2))
        cm = np.zeros((128, 16), np.float32)
        cm[:, 0:8] = (np.arange(8) < k).astype(np.float32)[None, :]
        if k > 0:
            cm[:, 8 + k - 1] = 1.0
        m["cmask"] = cm
        in_maps.append(m)

    if _PROBE_ONLY:
        return in_maps
    nc = build_nc()
    res = run_bass_kernel_spmd(nc, in_maps, core_ids=list(range(NCORE)))
    R = res.results

    y_prompt = np.empty((1, 16384, D), np.float32)
    y_sample = np.empty((32, 32, D), np.float32)
    conv_s = np.empty((L, 32, 2, 1024), np.float32)
    re_s = np.empty((L, 32, 64, 64), np.float32)
    im_s = np.empty((L, 32, 64, 64), np.float32)
    v_s = np.empty((L, 32, 32, 1024), np.float32)
    for k in range(NCORE):
        yc = R[k]["yT"].reshape(D, TC)
        for bk in range(NB):
            t0 = k * 2048 + bk * 512
            y_prompt[0, t0:t0 + 512] = yc[:, bk * BW:bk * BW + 512].T
            y_sample[4 * k + bk] = yc[:, bk * BW + 512:(bk + 1) * BW].T
        cs = R[k]["convs"]
        conv_s[:, 4 * k:4 * k + 4] = cs.transpose(0, 1, 4, 3, 2).reshape(L, NB, 2, 1024)
        ss_ = R[k]["ssms"]
        re_s[:, 4 * k:4 * k + 4] = ss_[:, 0].transpose(0, 3, 2, 1)
        im_s[:, 4 * k:4 * k + 4] = ss_[:, 1].transpose(0, 3, 2, 1)
        v_s[:, 4 * k:4 * k + 4] = R[k]["sguv"]
    cp = R[NCORE - 1]["convp"].reshape(L, 128, 8, 2)
    conv_p = np.ascontiguousarray(cp.transpose(0, 3, 2, 1).reshape(L, 1, 2, 1024))
    sp_ = R[NCORE - 1]["ssmp"]
    re_p = np.ascontiguousarray(sp_[:, 0].transpose(0, 2, 1)[:, None])
    im_p = np.ascontiguousarray(sp_[:, 1].transpose(0, 2, 1)[:, None])
    return (y_prompt, y_sample, conv_p, re_p, im_p, conv_s, re_s, im_s, v_s)
```
